# Optimizing a Trainium2 kernel written in Bass

```python
import jax, jax.numpy as jnp
from jax import lax
import numpy as np

D_MODEL = 1024
BATCH = 4
SEQ = 8192
DEPTH = 2
DEC_BATCH = 8
DEC_SEQ = 32
PAST_LEN = 1024

CHUNK = 64
EPS = 1e-6
N_HEADS = 16
N_KV_HEADS = 2
GROUP = N_HEADS // N_KV_HEADS
HEAD_DIM = 64
WINDOW = 128
WINDOW_CHUNKS = WINDOW // CHUNK
ATTN_WIDTH = N_HEADS * HEAD_DIM
KV_WIDTH = N_KV_HEADS * HEAD_DIM
POOL_WINDOWS = (2, 4, 8, 16)
POOL_GROUPS = 4
POOL_GROUP_WIDTH = D_MODEL // 8
POOL_WIDTH = POOL_GROUPS * POOL_GROUP_WIDTH
POOL_STATE = 16 - 1
SGU_LEN = 128
SGU_GROUPS = 4
SGU_GROUP_WIDTH = D_MODEL // 8
SGU_WIDTH = SGU_GROUPS * SGU_GROUP_WIDTH
N_BRANCH = 3
Q_END = ATTN_WIDTH
K_END = Q_END + KV_WIDTH
V_END = K_END + KV_WIDTH
POOL_END = V_END + POOL_WIDTH
SU_END = POOL_END + SGU_WIDTH
SV_END = SU_END + SGU_WIDTH
N_IN = SV_END + N_BRANCH * D_MODEL
IN_SPLITS = (Q_END, K_END, V_END, POOL_END, SU_END, SV_END)
D_FF = 2816
CONV_W = 3

kernel_name = "chunk_causal_hybrid_gated_encoder_step"


def rmsnorm(x, g):
    xf = x.astype(jnp.float32)
    y = xf * lax.rsqrt(jnp.mean(xf * xf, axis=-1, keepdims=True) + EPS)
    return (y * g.astype(jnp.float32)).astype(x.dtype)


def alibi_slopes():
    h = jnp.arange(1, N_HEADS + 1, dtype=jnp.float32)
    return jnp.exp2(-8.0 * h / N_HEADS)


def band_attention(q, k, v, qpos, kpos, sink):
    logits = jnp.einsum('bnqkgd,bnskd->bnkgqs', q, k).astype(jnp.float32) * (HEAD_DIM ** -0.5)
    qc = (qpos // CHUNK)[:, :, None]
    kc = (kpos // CHUNK)[:, None, :]
    vis = (kpos[:, None, :] >= 0) & (kc <= qc) & (qc - kc <= WINDOW_CHUNKS)
    dist = jnp.abs(qpos[:, :, None] - kpos[:, None, :]).astype(jnp.float32)
    slopes = alibi_slopes().reshape(N_KV_HEADS, GROUP)
    bias = jnp.where(vis[:, None, None], -slopes[None, :, :, None, None] * dist[:, None, None], -jnp.inf)
    logits = logits + bias[None]
    s = sink.astype(jnp.float32).reshape(N_KV_HEADS, GROUP)[None, None, :, :, None, None]
    m = jnp.maximum(jnp.max(logits, axis=-1, keepdims=True), s)
    p = jnp.exp(logits - m)
    denom = jnp.sum(p, axis=-1, keepdims=True) + jnp.exp(s - m)
    return jnp.einsum('bnkgqs,bnskd->bnqkgd', (p / denom).astype(v.dtype), v)


def attn_prompt(q, k, v, sink):
    B, T = q.shape[:2]
    nc = T // CHUNK
    pad = WINDOW_CHUNKS * CHUNK

    def band(a):
        ap = jnp.pad(a, ((0, 0), (pad, 0), (0, 0), (0, 0))).reshape(B, nc + WINDOW_CHUNKS, CHUNK, N_KV_HEADS, HEAD_DIM)
        return jnp.concatenate([ap[:, j:j + nc] for j in range(WINDOW_CHUNKS + 1)], axis=2)

    qb = q.reshape(B, nc, CHUNK, N_KV_HEADS, GROUP, HEAD_DIM)
    qpos = jnp.arange(T, dtype=jnp.int32).reshape(nc, CHUNK)
    kpos = (jnp.arange(nc, dtype=jnp.int32) * CHUNK - pad)[:, None] + jnp.arange((WINDOW_CHUNKS + 1) * CHUNK, dtype=jnp.int32)[None, :]
    o = band_attention(qb, band(k), band(v), qpos, kpos, sink)
    return o.reshape(B, T, ATTN_WIDTH)


def attn_sample(q, k, v, win_k, win_v, pos0, sink):
    B, T = q.shape[:2]
    wc = win_k.shape[1]
    k_all = jnp.concatenate([win_k.astype(k.dtype), k], axis=1)
    v_all = jnp.concatenate([win_v.astype(v.dtype), v], axis=1)
    qpos = (pos0 + jnp.arange(T, dtype=jnp.int32))[None]
    kpos = (pos0 - wc + jnp.arange(wc + T, dtype=jnp.int32))[None]
    o = band_attention(q[:, None], k_all[:, None], v_all[:, None], qpos, kpos, sink)
    return o.reshape(B, T, ATTN_WIDTH), k_all[:, T:], v_all[:, T:]


def pool_mixer(u_ext, pos0, pool_w, pool_scale):
    B, L, _ = u_ext.shape
    T = L - POOL_STATE
    uf = u_ext.astype(jnp.float32)
    cs = jnp.concatenate([jnp.zeros((B, 1, POOL_WIDTH), jnp.float32), jnp.cumsum(uf, axis=1)], axis=1)
    pos = pos0 + jnp.arange(T, dtype=jnp.int32)
    outs = []
    for gi, w in enumerate(POOL_WINDOWS):
        c0, c1 = gi * POOL_GROUP_WIDTH, (gi + 1) * POOL_GROUP_WIDTH
        win_sum = cs[:, POOL_STATE + 1:POOL_STATE + 1 + T, c0:c1] - cs[:, POOL_STATE + 1 - w:POOL_STATE + 1 - w + T, c0:c1]
        cnt = jnp.minimum(pos + 1, w).astype(jnp.float32)[None, :, None]
        d = (win_sum / cnt - uf[:, POOL_STATE:, c0:c1]).astype(u_ext.dtype)
        outs.append(jnp.einsum('btc,cd->btd', d, pool_w[gi]))
    return jnp.concatenate(outs, axis=-1) * pool_scale


def spatial_gate(u, vn, sgu_w, sgu_b):
    L = u.shape[2]
    i = jnp.arange(L)
    mask = (i[None, :] // CHUNK) <= (i[:, None] // CHUNK)
    w = jnp.where(mask[None], sgu_w[:, :L, :L], 0)
    s = jnp.einsum('gij,bnjgc->bnigc', w, vn) + jnp.transpose(sgu_b[:, :L])[None, None, :, :, None]
    return u * s


def conv_ffn(h, conv_prefix, ffn_w_up, ffn_conv_w, ffn_conv_b, ffn_w_down):
    T = h.shape[1]
    up = jnp.einsum('btd,df->btf', h, ffn_w_up)
    a, b = jnp.split(up, 2, axis=-1)
    a_ext = jnp.concatenate([conv_prefix.astype(a.dtype), a], axis=1)
    conv = sum(a_ext[:, j:j + T] * ffn_conv_w[j] for j in range(CONV_W)) + ffn_conv_b
    out = jnp.einsum('btf,fd->btd', jax.nn.gelu(conv) * b, ffn_w_down)
    return out, a_ext[:, -(CONV_W - 1):]


def layer_forward(x, c, win_k, win_v, pool_prefix, conv_prefix, pos0,
                  norm1_g, norm2_g, w_ada, b_ada, w_in, attn_sink, w_o_attn,
                  pool_w, pool_scale, w_o_pool, sgu_norm_g, sgu_w, sgu_b, w_o_sgu,
                  w_out, ffn_w_up, ffn_conv_w, ffn_conv_b, ffn_w_down):
    B, T, _ = x.shape
    mod = jnp.einsum('bd,de->be', jax.nn.silu(c), w_ada) + b_ada
    sh1, sc1, g1, sh2, sc2, g2 = [m[:, None, :] for m in jnp.split(mod, 6, axis=-1)]
    h = rmsnorm(x, norm1_g) * (1 + sc1) + sh1
    z = jnp.einsum('btd,de->bte', h, w_in)
    q, k, v, pu, su, sv, gates = jnp.split(z, IN_SPLITS, axis=-1)
    q = q.reshape(B, T, N_KV_HEADS, GROUP, HEAD_DIM)
    k = k.reshape(B, T, N_KV_HEADS, HEAD_DIM)
    v = v.reshape(B, T, N_KV_HEADS, HEAD_DIM)
    if win_k is None:
        ya = attn_prompt(q, k, v, attn_sink)
        keep = min(WINDOW, T)
        new_k, new_v = k[:, T - keep:], v[:, T - keep:]
    else:
        ya, new_k, new_v = attn_sample(q, k, v, win_k, win_v, pos0, attn_sink)
    u_ext = jnp.concatenate([pool_prefix.astype(pu.dtype), pu], axis=1)
    yb = pool_mixer(u_ext, pos0, pool_w, pool_scale)
    new_pool = u_ext[:, -POOL_STATE:]
    uu = jax.nn.gelu(su)
    vn = rmsnorm(jax.nn.gelu(sv), sgu_norm_g)
    L = SGU_LEN if win_k is None else T
    yc = spatial_gate(uu.reshape(B, T // L, L, SGU_GROUPS, SGU_GROUP_WIDTH),
                      vn.reshape(B, T // L, L, SGU_GROUPS, SGU_GROUP_WIDTH),
                      sgu_w, sgu_b).reshape(B, T, SGU_WIDTH)
    ga, gb, gc = jnp.split(gates, N_BRANCH, axis=-1)
    merged = (jax.nn.sigmoid(ga) * jnp.einsum('bte,ed->btd', ya, w_o_attn)
              + jax.nn.sigmoid(gb) * jnp.einsum('bte,ed->btd', yb, w_o_pool)
              + jax.nn.sigmoid(gc) * jnp.einsum('bte,ed->btd', yc, w_o_sgu))
    x = x + g1 * jnp.einsum('btd,de->bte', merged, w_out)
    h2 = rmsnorm(x, norm2_g) * (1 + sc2) + sh2
    f, new_conv = conv_ffn(h2, conv_prefix, ffn_w_up, ffn_conv_w, ffn_conv_b, ffn_w_down)
    x = x + g2 * f
    return x, new_k, new_v, new_pool, new_conv, vn


def setup_inputs(seed: int = 0) -> dict:
    key = jax.random.key(seed)
    ks = jax.random.split(key, 28)

    def nrm(k, shape, scale):
        return jax.random.normal(k, shape, jnp.float32) * scale

    win_cache = min(WINDOW, PAST_LEN)
    return {
        "x_prompt": nrm(ks[0], (BATCH, SEQ, D_MODEL), 1.0),
        "x_sample": nrm(ks[1], (DEC_BATCH, DEC_SEQ, D_MODEL), 1.0),
        "c_prompt": nrm(ks[2], (BATCH, D_MODEL), 1.0),
        "c_sample": nrm(ks[3], (DEC_BATCH, D_MODEL), 1.0),
        "cache_k_win": nrm(ks[4], (DEPTH, DEC_BATCH, win_cache, N_KV_HEADS, HEAD_DIM), 1.0),
        "cache_v_win": nrm(ks[5], (DEPTH, DEC_BATCH, win_cache, N_KV_HEADS, HEAD_DIM), 1.0),
        "state_pool": nrm(ks[6], (DEPTH, DEC_BATCH, POOL_STATE, POOL_WIDTH), 1.0),
        "state_ffn_conv": nrm(ks[7], (DEPTH, DEC_BATCH, CONV_W - 1, D_FF), 1.0),
        "norm1_g": 1.0 + nrm(ks[8], (DEPTH, D_MODEL), 0.05),
        "norm2_g": 1.0 + nrm(ks[9], (DEPTH, D_MODEL), 0.05),
        "w_ada": nrm(ks[10], (DEPTH, D_MODEL, 6 * D_MODEL), 0.5 * D_MODEL ** -0.5),
        "b_ada": nrm(ks[11], (DEPTH, 6 * D_MODEL), 0.02),
        "w_in": nrm(ks[12], (DEPTH, D_MODEL, N_IN), D_MODEL ** -0.5),
        "attn_sink": nrm(ks[13], (DEPTH, N_HEADS), 0.5),
        "w_o_attn": nrm(ks[14], (DEPTH, ATTN_WIDTH, D_MODEL), ATTN_WIDTH ** -0.5),
        "pool_w": nrm(ks[15], (DEPTH, POOL_GROUPS, POOL_GROUP_WIDTH, POOL_GROUP_WIDTH), POOL_GROUP_WIDTH ** -0.5),
        "pool_scale": 1.0 + nrm(ks[16], (DEPTH, POOL_WIDTH), 0.1),
        "w_o_pool": nrm(ks[17], (DEPTH, POOL_WIDTH, D_MODEL), POOL_WIDTH ** -0.5),
        "sgu_norm_g": 1.0 + nrm(ks[18], (DEPTH, SGU_WIDTH), 0.05),
        "sgu_w": nrm(ks[19], (DEPTH, SGU_GROUPS, SGU_LEN, SGU_LEN), SGU_LEN ** -0.5),
        "sgu_b": 1.0 + nrm(ks[20], (DEPTH, SGU_GROUPS, SGU_LEN), 0.1),
        "w_o_sgu": nrm(ks[21], (DEPTH, SGU_WIDTH, D_MODEL), SGU_WIDTH ** -0.5),
        "w_out": nrm(ks[22], (DEPTH, D_MODEL, D_MODEL), D_MODEL ** -0.5),
        "ffn_w_up": nrm(ks[23], (DEPTH, D_MODEL, 2 * D_FF), D_MODEL ** -0.5),
        "ffn_conv_w": nrm(ks[24], (DEPTH, CONV_W, D_FF), CONV_W ** -0.5),
        "ffn_conv_b": nrm(ks[25], (DEPTH, D_FF), 0.02),
        "ffn_w_down": nrm(ks[26], (DEPTH, D_FF, D_MODEL), D_FF ** -0.5),
        "final_norm_g": 1.0 + nrm(ks[27], (D_MODEL,), 0.05),
    }


def reference(x_prompt, x_sample, c_prompt, c_sample, cache_k_win, cache_v_win, state_pool, state_ffn_conv,
              norm1_g, norm2_g, w_ada, b_ada, w_in, attn_sink, w_o_attn, pool_w, pool_scale, w_o_pool,
              sgu_norm_g, sgu_w, sgu_b, w_o_sgu, w_out, ffn_w_up, ffn_conv_w, ffn_conv_b, ffn_w_down,
              final_norm_g):
    layer_params = (norm1_g, norm2_g, w_ada, b_ada, w_in, attn_sink, w_o_attn, pool_w, pool_scale, w_o_pool,
                    sgu_norm_g, sgu_w, sgu_b, w_o_sgu, w_out, ffn_w_up, ffn_conv_w, ffn_conv_b, ffn_w_down)
    xp, xs = x_prompt, x_sample
    kp, vp, pp, cp = [], [], [], []
    ks_, vs_, ps_, cs_, ss_ = [], [], [], [], []
    for l in range(DEPTH):
        lw = [p[l] for p in layer_params]
        zero_pool = jnp.zeros((xp.shape[0], POOL_STATE, POOL_WIDTH), xp.dtype)
        zero_conv = jnp.zeros((xp.shape[0], CONV_W - 1, D_FF), xp.dtype)
        xp, nk, nv, npool, nconv, _ = layer_forward(xp, c_prompt, None, None, zero_pool, zero_conv, 0, *lw)
        kp.append(nk); vp.append(nv); pp.append(npool); cp.append(nconv)
        xs, nk, nv, npool, nconv, nsv = layer_forward(xs, c_sample, cache_k_win[l], cache_v_win[l],
                                                     state_pool[l], state_ffn_conv[l], PAST_LEN, *lw)
        ks_.append(nk); vs_.append(nv); ps_.append(npool); cs_.append(nconv); ss_.append(nsv)
    y_prompt = rmsnorm(xp, final_norm_g)
    y_sample = rmsnorm(xs, final_norm_g)
    return (y_prompt, y_sample,
            jnp.stack(kp), jnp.stack(vp), jnp.stack(pp), jnp.stack(cp),
            jnp.stack(ks_), jnp.stack(vs_), jnp.stack(ps_), jnp.stack(cs_), jnp.stack(ss_))
```

```python
import numpy as np
from contextlib import ExitStack
import concourse.bass as bass
import concourse.mybir as mybir
from concourse.bass_utils import run_bass_kernel_spmd

F32 = mybir.dt.float32
BF16 = mybir.dt.bfloat16
AF = mybir.ActivationFunctionType
ALU = mybir.AluOpType

D = 1024
KT = 8
DFF = 2816
FT = 22
NIN = 5888
EPS = 1e-6
HALO = 384
HALF = 4096
NTOK = HALO + HALF
NTILES = 9
NG_L = 36
NBUF = 4
S_N1G, S_N2G, S_PSC, S_CW, S_CB, S_BADA, S_SINK, S_SNG, S_SGB = 0, 8, 16, 20, 86, 108, 156, 172, 684
NS = 1196

ENGS = ("pe", "act", "dve", "pool", "sp")


class Res:
    __slots__ = ("name", "excl", "last_w", "readers")

    def __init__(self, name, excl=False):
        self.name = name
        self.excl = excl
        self.last_w = None
        self.readers = []


class Op:
    __slots__ = ("eng", "idx", "fn", "waits", "needs_inc", "inc_val", "dma", "dma_val")

    def __init__(self, eng, idx, fn):
        self.eng = eng
        self.idx = idx
        self.fn = fn
        self.waits = []
        self.needs_inc = False
        self.inc_val = None
        self.dma = None
        self.dma_val = None


class DmaSlot:
    def __init__(self, prog, name):
        self.name = name
        self.count = 0
        self.sem = None
        prog.slots.append(self)


class Prog:
    def __init__(self):
        self.ops = {e: [] for e in ENGS}
        self.slots = []
        self.waited = {e: {} for e in ENGS}
        self.final_dma = []

    def _deps(self, reads, writes):
        deps = []
        for r in reads:
            if r.excl:
                if r.last_w is not None:
                    deps.append(r.last_w)
                deps.extend(r.readers)
            elif r.last_w is not None:
                deps.append(r.last_w)
        for w in writes:
            if w.last_w is not None:
                deps.append(w.last_w)
            deps.extend(w.readers)
        return deps

    def _commit(self, op, reads, writes):
        for r in reads:
            if r.excl:
                r.last_w = op
                r.readers = []
            else:
                r.readers.append(op)
                if len(r.readers) > 64:
                    r.readers = r.readers[-48:]
        for w in writes:
            w.last_w = op
            w.readers = []

    def _add_waits(self, op, deps):
        need = {}
        for d in deps:
            if d is op:
                continue
            if d.dma is not None:
                key = ("dma", d.dma)
                need[key] = max(need.get(key, 0), d.dma_val)
            else:
                if d.eng == op.eng and op.dma is None:
                    if op.eng == "pe":
                        continue
                key = ("eng", d.eng)
                cur = need.get(key)
                if cur is None or d.idx > cur.idx:
                    need[key] = d
        wd = self.waited[op.eng]
        for key, v in need.items():
            if key[0] == "dma":
                if wd.get(key, 0) >= v:
                    continue
                wd[key] = v
                op.waits.append(("dma", key[1], v))
            else:
                prev = wd.get(key, -1)
                if prev >= v.idx:
                    continue
                wd[key] = v.idx
                v.needs_inc = True
                op.waits.append(("eng", v, None))

    def op(self, eng, fn, reads=(), writes=()):
        o = Op(eng, len(self.ops[eng]), fn)
        self._add_waits(o, self._deps(reads, writes))
        self._commit(o, reads, writes)
        self.ops[eng].append(o)
        return o

    def dma(self, queue, slot, fn, reads=(), writes=(), final=False):
        o = Op(queue, len(self.ops[queue]), fn)
        o.dma = slot
        if final and slot.count > 0:
            o.waits.append(("dma", slot, slot.count))
        slot.count += 16
        o.dma_val = slot.count
        self._add_waits(o, self._deps(reads, writes))
        self._commit(o, reads, writes)
        self.ops[queue].append(o)
        if final:
            self.final_dma.append(o)
        return o

    def emit(self, nc, stack):
        esem = {}
        for e in ENGS:
            esem[e] = stack.enter_context(nc.semaphore("sem_" + e))
            c = 0
            for o in self.ops[e]:
                if o.dma is None and o.needs_inc:
                    c += 1
                    o.inc_val = c
        for i, s in enumerate(self.slots):
            if s.count > 0:
                s.sem = stack.enter_context(nc.semaphore("dsem%d" % i))
        block = stack.enter_context(nc.Block())

        def run(ename):
            def body(e):
                for o in self.ops[ename]:
                    for (kind, a, v) in o.waits:
                        if kind == "dma":
                            e.wait_ge(a.sem, v)
                        else:
                            e.wait_ge(esem[a.eng], a.inc_val)
                    ins = o.fn(e)
                    if o.dma is not None:
                        ins.then_inc(o.dma.sem, 16)
                    elif o.needs_inc:
                        ins.then_inc(esem[ename], 1)
                if ename == "sp":
                    fin = {}
                    for o in self.final_dma:
                        fin[o.dma] = max(fin.get(o.dma, 0), o.dma_val)
                    for s, v in fin.items():
                        e.wait_ge(s.sem, v)
            return body

        block.tensor(run("pe"))
        block.scalar(run("act"))
        block.vector(run("dve"))
        block.gpsimd(run("pool"))
        block.sync(run("sp"))


class Seg:
    def __init__(self, kind, c0, n, r):
        self.kind, self.c0, self.n, self.r = kind, c0, n, r


def tile_segs(t):
    if t == 0:
        return [Seg("P", 0, HALO, 0), Seg("S", HALO, 32, 1)]
    return [Seg("P", 0, 512, 0)]


def build_program():
    nc = bass.Bass("TRN2", target_bir_lowering=False)
    P = Prog()
    st = ExitStack()

    def din(name, shape):
        return nc.dram_tensor(name, list(shape), F32, kind="ExternalInput").ap()

    def dout(name, shape):
        return nc.dram_tensor(name, list(shape), F32, kind="ExternalOutput").ap()

    xTp = din("xTp", [D, NTOK])
    xTs = din("xTs", [D, 32])
    cT_d = din("cT", [128, KT, 2])
    wl_d = din("wl", [2, NG_L, 128, 4096])
    wada_d = din("wada", [2, 12, 128, 4096])
    smalls_d = din("smalls", [2, 128, NS])
    fng_d = din("fng", [128, KT])
    poolw_d = din("poolw", [2, 128, 4, 128])
    sguwT_d = din("sguwT", [2, 128, 4, 128])
    sgumask_d = din("sgumask", [128, 128])
    eb_d = din("eb", [64, 6, 512])
    coremask_d = din("coremask", [128, 1])
    poolinv_d = din("poolinv", [128, 4, 16])
    kcz_d = din("kcz", [2, 128, 4, 128])
    vcd_d = din("vcd", [2, 64, 2, 256])
    cktok_d = din("cktok", [2, 128, 128])
    cvtok_d = din("cvtok", [2, 128, 128])
    spoolT_d = din("spoolT", [2, 128, 4, 15])
    sconvT_d = din("sconvT", [2, 128, FT, 2])

    yT_d = dout("yT", [D, HALF])
    ysT_d = dout("ysT", [D, 32])
    okp_d = dout("okp", [2, 128, 128])
    ovp_d = dout("ovp", [2, 128, 128])
    opoolp_d = dout("opoolp", [2, 15, 512])
    oconvp_d = dout("oconvp", [2, 2, DFF])
    oks_d = dout("oks", [2, 128, 128])
    ovs_d = dout("ovs", [2, 128, 128])
    opools_d = dout("opools", [2, 15, 512])
    oconvs_d = dout("oconvs", [2, 2, DFF])
    osgu_d = dout("osgu", [2, 32, 512])

    def sb(name, shape, dt):
        return st.enter_context(nc.sbuf_tensor("sb_" + name, list(shape), dt))

    xTa = sb("xTa", [128, KT, 512], F32)
    xTb = sb("xTb", [128, KT, 512], F32)
    xT = xTa
    hT = sb("hT", [128, KT, 512], BF16)
    qT = sb("qT", [128, KT, 512], BF16)
    yaT = sb("yaT", [128, KT, 512], BF16)
    x2 = yaT
    G = sb("G", [128, 24, 512], BF16)
    KKb = sb("KKb", [128, 4, 128 + 512], BF16)
    KKst = [sb("KKst%d" % l, [128, 4, 128], BF16) for l in range(2)]
    KKs = sb("KKs", [128, 4, 32], BF16)
    KKc = [sb("KKc%d" % l, [128, 4, 128], BF16) for l in range(2)]
    V64b = sb("V64b", [128, 10, 256], BF16)
    V64st = [sb("V64st%d" % l, [128, 2, 256], BF16) for l in range(2)]
    V64s = sb("V64s", [64, 256], BF16)
    Vc = [sb("Vc%d" % l, [128, 2, 256], BF16) for l in range(2)]
    puTb = sb("puTb", [128, 4, 16 + 512], F32)
    pust = [sb("pust%d" % l, [128, 4, 16], F32) for l in range(2)]
    puTs = [sb("puTs%d" % l, [128, 4, 16 + 32], F32) for l in range(2)]
    dT = sb("dT", [128, 4, 512], BF16)
    ybT = sb("ybT", [128, 4, 512], BF16)
    uuT = sb("uuT", [128, 4, 512], BF16)
    vn = sb("vn", [128, 4, 512], BF16)
    aprev = [sb("aprev%d" % l, [128, FT, 2], F32) for l in range(2)]
    aprev_s = [sb("aprevs%d" % l, [128, FT, 2], F32) for l in range(2)]
    NSF, NSB = 6, 0
    scrF = [sb("scrF%d" % i, [128, 528], F32) for i in range(NSF)]
    scrB = [sb("scrB%d" % i, [128, 512], BF16) for i in range(NSB)]
    wbuf = [sb("wbuf%d" % i, [128, 4096], BF16) for i in range(NBUF)]
    sm = [sb("sm%d" % l, [128, NS], F32) for l in range(2)]
    fng = sb("fng", [128, KT], F32)
    poolw = [sb("poolw%d" % l, [128, 4, 128], BF16) for l in range(2)]
    sguw = [sb("sguw%d" % l, [128, 4, 128], BF16) for l in range(2)]
    sgumask = sb("sgumask", [128, 128], F32)
    eb = sb("eb", [64, 6, 512], BF16)
    coremask = sb("coremask", [128, 1], F32)
    poolinv = sb("poolinv", [128, 4, 16], F32)
    ones_bf = sb("ones_bf", [128, 128], BF16)
    sel = sb("sel", [128, 2, 128], BF16)
    es2 = [sb("es2_%d" % l, [128, 2, 4], F32) for l in range(2)]
    cT = sb("cT", [128, KT, 2], F32)
    scT = sb("scT", [128, KT, 2], BF16)
    modT = [sb("modT%d" % l, [128, 48, 2], F32) for l in range(2)]
    s1 = [sb("s1_%d" % l, [128, KT, 2], F32) for l in range(2)]
    s2 = [sb("s2_%d" % l, [128, KT, 2], F32) for l in range(2)]
    g1h = [sb("g1h_%d" % l, [128, KT, 2], F32) for l in range(2)]
    es = [sb("es%d" % l, [128, 16], F32) for l in range(2)]
    ss4 = sb("ss4", [128, 8], F32)
    rs4 = sb("rs4", [128, 8], F32)
    psum = [st.enter_context(nc.psum_tensor("ps%d" % i, [128, 512], F32)) for i in range(8)]

    R_xa = [Res("xa%d" % k) for k in range(KT)]
    R_xb = [Res("xb%d" % k) for k in range(KT)]
    R_x = R_xa
    R_h = [Res("h%d" % k) for k in range(KT)]
    R_q = [Res("q%d" % k) for k in range(KT)]
    R_ya = [Res("ya%d" % k) for k in range(KT)]
    R_x2 = R_ya
    R_G = [Res("G%d" % k) for k in range(24)]
    R_KKb = Res("KKb")
    R_KKst = [Res("KKst%d" % l) for l in range(2)]
    R_KKs = Res("KKs")
    R_Vb = Res("Vb")
    R_Vst = [Res("Vst%d" % l) for l in range(2)]
    R_Vs = Res("Vs")
    R_pub = [Res("pub%d" % g) for g in range(4)]
    R_pust = [Res("pust%d" % l) for l in range(2)]
    R_pus = [[Res("pus%d_%d" % (l, g)) for g in range(4)] for l in range(2)]
    R_d = [Res("d%d" % g) for g in range(4)]
    R_yb = [Res("yb%d" % g) for g in range(4)]
    R_uu = [Res("uu%d" % g) for g in range(4)]
    R_vn = [Res("vn%d" % g) for g in range(4)]
    R_ap = [Res("ap%d" % l) for l in range(2)]
    R_aps = [Res("aps%d" % l) for l in range(2)]
    R_sF = [Res("sF%d" % i) for i in range(NSF)]
    R_sB = [Res("sB%d" % i) for i in range(NSB)]
    R_w = [Res("w%d" % i) for i in range(NBUF)]
    R_ps = [Res("ps%d" % i, excl=True) for i in range(8)]
    R_const = Res("const")
    R_mod = [Res("mod%d" % l) for l in range(2)]
    R_ss4 = Res("ss4")
    R_rs4 = Res("rs4")

    cnt = {"F": 0, "B": 0, "ps": 0}

    def gF():
        i = cnt["F"] % NSF
        cnt["F"] += 1
        return scrF[i], R_sF[i]

    def gB():
        i = cnt["B"] % NSB
        cnt["B"] += 1
        return scrB[i], R_sB[i]

    def gP():
        i = cnt["ps"] % 8
        cnt["ps"] += 1
        return psum[i], R_ps[i]

    def mm(out, lhsT, rhs, start, stop, reads, writes):
        P.op("pe", lambda e, o=out, l=lhsT, r=rhs, s=start, t=stop: e.matmul(o, lhsT=l, rhs=r, start=s, stop=t),
             reads, writes)

    def act(out, in_, func, reads, writes, bias=None, scale=None, accum_out=None):
        kw = {}
        if bias is not None:
            kw["bias"] = bias
        if scale is not None:
            kw["scale"] = scale
        if accum_out is not None:
            kw["accum_out"] = accum_out
        P.op("act", lambda e, o=out, i=in_, f=func, kw=kw: e.activation(out=o, in_=i, func=f, **kw), reads, writes)

    def tt(out, in0, in1, op, reads, writes, eng="dve"):
        P.op(eng, lambda e, o=out, a=in0, b=in1, p=op: e.tensor_tensor(out=o, in0=a, in1=b, op=p), reads, writes)

    def stt(out, in0, scalar, in1, op0, op1, reads, writes):
        P.op("dve", lambda e, o=out, a=in0, s=scalar, b=in1, p0=op0, p1=op1:
             e.scalar_tensor_tensor(out=o, in0=a, scalar=s, in1=b, op0=p0, op1=p1), reads, writes)

    def tsc(out, in0, s1_, s2_, op0, op1, reads, writes):
        if s2_ is None:
            P.op("dve", lambda e, o=out, a=in0, s=s1_, p0=op0: e.tensor_scalar(out=o, in0=a, scalar1=s, scalar2=None, op0=p0),
                 reads, writes)
        else:
            P.op("dve", lambda e, o=out, a=in0, s=s1_, s2v=s2_, p0=op0, p1=op1:
                 e.tensor_scalar(out=o, in0=a, scalar1=s, scalar2=s2v, op0=p0, op1=p1), reads, writes)

    def cpy(eng, out, in_, reads, writes):
        if eng == "act":
            act(out, in_, AF.Copy, reads, writes)
        else:
            P.op(eng, lambda e, o=out, i=in_: e.tensor_copy(out=o, in_=i), reads, writes)

    def memset(eng, ap, val, writes):
        P.op(eng, lambda e, a=ap, v=val: e.memset(a, v), (), writes)

    oslots = []
    ocnt = [0]

    def dma(queue, slot, out, in_, reads, writes, final=False):
        if final:
            if not oslots:
                oslots.extend(DmaSlot(P, "out%d" % i) for i in range(8))
            slot = oslots[ocnt[0] % 8]
            ocnt[0] += 1
        return P.dma(queue, slot, lambda e, o=out, i=in_: e.dma_start(out=o, in_=i), reads, writes, final=final)

    wslots = [DmaSlot(P, "w%d" % i) for i in range(NBUF)]
    sched = []
    for g in range(4):
        sched.append((wada_d[0, g], 4096))

    def layer_sched(l):
        out = []
        for g in range(NG_L):
            n = 4096
            if g == 3:
                n = -256
            if g == NG_L - 1:
                n = 2048
            out.append((wl_d[l, g], n))
        return out

    L1_SLOTS = [8, 10, 11, 13, 15, 17, 19, 21, 23, 25, 27, 29]

    def ilv_plan(gi, n1):
        if gi < 8:
            return (0, 4 + gi)
        if gi in L1_SLOTS:
            return (1, L1_SLOTS.index(gi))
        return None

    for t in range(NTILES):
        for l in range(2):
            grp = layer_sched(l)
            if t == 0 and l == 0:
                n1 = 0
                for gi, g_ in enumerate(grp):
                    sched.append(g_)
                    pl = ilv_plan(gi, n1)
                    if pl is not None:
                        sched.append((wada_d[pl[0], pl[1]], 4096))
                        if pl[0] == 1:
                            n1 += 1
                assert n1 == 12
            else:
                sched.extend(grp)
    wstate = {"issued": 0, "used": 0}

    def w_issue():
        i = wstate["issued"]
        if i >= len(sched):
            return
        src, n = sched[i]
        s = i % NBUF
        if n == -256:
            o = wbuf[s][:].rearrange("p (k c) -> p k c", k=8)[:, :, 0:256]
            src_ap = src.rearrange("p (k c) -> p k c", k=8)[:, :, 0:256]
        else:
            o = wbuf[s][:, 0:n]
            src_ap = src[:, 0:n]
        dma("pool", wslots[s], o, src_ap, (), [R_w[s]])
        wstate["issued"] += 1

    def w_next():
        i = wstate["used"]
        assert i < wstate["issued"]
        s = i % NBUF
        wstate["used"] += 1
        return wbuf[s], R_w[s]

    ilv = {"active": False, "gi": 0, "n1": 0}

    def w_done():
        w_issue()
        if ilv["active"]:
            gi = ilv["gi"]
            ilv["gi"] += 1
            pl = ilv_plan(gi, ilv["n1"])
            if pl is not None:
                mod_group(pl[0], pl[1])
                if pl[0] == 1:
                    ilv["n1"] += 1

    for _ in range(NBUF):
        w_issue()

    ld = [DmaSlot(P, "ld%d" % i) for i in range(4)]
    for l in range(2):
        dma("sp", ld[0], sm[l][:], smalls_d[l], (), [R_const])
    dma("sp", ld[0], fng[:], fng_d[:, :], (), [R_const])
    dma("sp", ld[0], sgumask[:], sgumask_d[:, :], (), [R_const])
    dma("sp", ld[0], coremask[:], coremask_d[:, :], (), [R_const])
    dma("sp", ld[0], poolinv[:], poolinv_d[:, :, :], (), [R_const])
    dma("sp", ld[0], cT[:], cT_d[:, :, :], (), [R_const])
    R_setup = Res("setup")
    for l in range(2):
        memset("dve", puTs[l][:, :, 0:1], 0.0, R_pus[l])
        dma("sp", DmaSlot(P, "stp%d" % l), puTs[l][:, :, 1:16], spoolT_d[l], (), [R_pus[l][0], R_pus[l][1], R_pus[l][2], R_pus[l][3]])
        dma("sp", DmaSlot(P, "stc%d" % l), aprev_s[l][:], sconvT_d[l], (), [R_aps[l]])
    dma("pool", ld[2], eb[:], eb_d[:, :, :], (), [R_setup])
    for l in range(2):
        dma("pool", ld[2], poolw[l][:], poolw_d[l], (), [R_setup])
        dma("pool", ld[2], KKc[l][:], kcz_d[l], (), [R_setup])
        memset("dve", Vc[l][64:128, :, :], 0.0, [R_setup])
        dma("pool", ld[2], Vc[l][0:64, :, :], vcd_d[l], (), [R_setup])
    so = None
    for l in range(2):
        dma("sp", so, oks_d[l, 0:96, :], cktok_d[l, 32:128, :], (), (), final=True)
        dma("sp", so, ovs_d[l, 0:96, :], cvtok_d[l, 32:128, :], (), (), final=True)

    memset("dve", ones_bf[:], 1.0, [R_const])
    memset("dve", sel[:], 0.0, [R_const])
    memset("dve", sel[:, 0, 0:64], 1.0, [R_const])
    memset("dve", sel[:, 1, 64:128], 1.0, [R_const])
    for l in range(2):
        memset("dve", aprev[l][:], 0.0, [R_ap[l]])
        memset("dve", pust[l][:], 0.0, [R_pust[l]])
        memset("dve", KKst[l][:], 0.0, [R_KKst[l]])
        memset("dve", V64st[l][:], 0.0, [R_Vst[l]])
    memset("dve", V64b[:], 0.0, [R_Vb])
    for l in range(2):
        sguw_f = scrF[0][:, 0:512].rearrange("p (g i) -> p g i", g=4)
        dma("sp", ld[3], sguw_f, sguwT_d[l], (), [R_sF[0]])
        for g in range(4):
            tt(sguw[l][:, g, :], sguw_f[:, g, :], sgumask[:], ALU.mult, [R_sF[0], R_const], [R_setup])
    act(scT[:], cT[:], AF.Silu, [R_const], [R_setup])
    R_modA = [Res("modA%d" % l) for l in range(2)]

    def mod_group(l, g):
        wb, rw = w_next()
        wv = wb[:].rearrange("p (k c) -> p k c", k=8)
        pb, rpb = gP()
        for f4 in range(4):
            for k in range(KT):
                mm(pb[:, 2 * f4:2 * f4 + 2], wv[:, k, f4 * 128:(f4 + 1) * 128], scT[:, k, :], k == 0, k == KT - 1,
                   [rw, R_setup], [rpb])
        w_issue()
        rm = R_modA[l] if g < 4 else R_mod[l]
        bada = sm[l][:, S_BADA + 4 * g:S_BADA + 4 * g + 4]
        tt(modT[l][:, 4 * g:4 * g + 4, :], pb[:, 0:8].rearrange("p (f r) -> p f r", r=2),
           bada.unsqueeze(2).to_broadcast([128, 4, 2]), ALU.add, [rpb, R_const], [rm])
        if g == 3:
            n1g = sm[l][:, S_N1G:S_N1G + 8].unsqueeze(2).to_broadcast([128, 8, 2])
            stt(s1[l][:], modT[l][:, 8:16, :], 1.0, n1g, ALU.add, ALU.mult, [R_modA[l], R_const], [R_modA[l]])
            act(es[l][:], sm[l][:, S_SINK:S_SINK + 16], AF.Exp, [R_const], [R_modA[l]])
            esv_ = es[l][:].rearrange("p (g e i) -> p g e i", g=2, e=2)
            cpy("dve", es2[l][0:64, :, :], esv_[0:64, :, 0, :], [R_modA[l]], [R_modA[l]])
            cpy("dve", es2[l][64:128, :, :], esv_[64:128, :, 1, :], [R_modA[l]], [R_modA[l]])
        if g == 11:
            n2g = sm[l][:, S_N2G:S_N2G + 8].unsqueeze(2).to_broadcast([128, 8, 2])
            stt(s2[l][:], modT[l][:, 32:40, :], 1.0, n2g, ALU.add, ALU.mult, [R_mod[l], R_const], [R_mod[l]])
            tsc(g1h[l][:], modT[l][:, 16:24, :], 0.5, None, ALU.mult, None, [R_mod[l]], [R_mod[l]])

    for g in range(4):
        mod_group(0, g)
    xslot = DmaSlot(P, "xin")

    def norm_phase(Tt, segs, svec, sh_base, l):
        act(tlj[:, 0:1], epsb[:, 0:1], AF.Ln, [R_const], [R_tlj])
        for k in range(KT):
            act(x2[:, k, 0:Tt], xT[:, k, 0:Tt], AF.Square, [R_x[k]], [R_x2[k]])
        pb, rpb = gP()
        for k in range(KT):
            mm(pb[:, 0:Tt], ones_bf[:], x2[:, k, 0:Tt], k == 0, k == KT - 1, [R_x2[k], R_const], [rpb])
        rb, rr = rstdb, R_rstd
        act(rb[:, 0:Tt], pb[:, 0:Tt], AF.Ln, [rpb, R_const], [rr], bias=epsb[:, 0:1], scale=1.0 / D)
        act(rb[:, 0:Tt], rb[:, 0:Tt], AF.Exp, [rr], [rr], scale=-0.5)
        for k in range(KT):
            tb, rt = gF()
            for sg in segs:
                cs = slice(sg.c0, sg.c0 + sg.n)
                rmod = R_modA[l] if sh_base == 0 else R_mod[l]
                stt(tb[:, cs], xT[:, k, cs], svec[:, k, sg.r:sg.r + 1], rb[:, cs], ALU.mult, ALU.mult,
                    [R_x[k], rr, rmod], [rt])
                act(hT[:, k, cs], tb[:, cs], AF.Identity, [rt, rmod], [R_h[k]],
                    bias=modT[l][:, sh_base + k, sg.r:sg.r + 1])

    rstdb = sb("rstdb", [128, 512], F32)
    R_rstd = Res("rstd")
    tlj = sb("tlj", [128, 1], F32)
    R_tlj = Res("tlj")
    epsb = sb("epsb", [128, 1], F32)
    memset("dve", epsb[:], EPS, [R_const])

    def proj_fm(wv, rw, col0, Tt):
        pb, rpb = gP()
        for k in range(KT):
            mm(pb[:, 0:Tt], wv[:, k, col0:col0 + 128], hT[:, k, 0:Tt], k == 0, k == KT - 1, [rw, R_h[k]], [rpb])
        return pb, rpb

    def tok_rows(c0, n, rhs_of_k, N, rw):
        pb, rpb = gP()
        for k in range(KT):
            mm(pb[0:n, 0:N], hT[:, k, c0:c0 + n], rhs_of_k(k), k == 0, k == KT - 1, [rw, R_h[k]], [rpb])
        return pb, rpb

    NSP = 6
    scrP = [sb("scrP%d" % i, [128, 512], BF16) for i in range(NSP)]
    R_sP = [Res("sP%d" % i) for i in range(NSP)]
    for i in range(NSP):
        memset("dve", scrP[i][:], 0.0, [R_sP[i]])
    acnt = {"P": 0, "S": 0, "OD": 0}

    def attn_S(l, g, qc0, nq, pieces):
        W = 8 * nq
        plist = []
        for pi, (kk0, kk1, vap, nk, kind, rlist, masked) in enumerate(pieces):
            bi = acnt["S"] % 2
            acnt["S"] += 1
            sbk, rsb = psum[bi], R_ps[bi]
            for e, kk in ((0, kk0), (1, kk1)):
                mm(sbk[0:nk, e * 4 * nq:(e + 1) * 4 * nq], kk, qT[:, 4 * g:4 * g + 4, qc0:qc0 + nq], True, True,
                   rlist + [R_q[4 * g + i] for i in range(4)], [rsb])
            i = acnt["P"] % NSP
            acnt["P"] += 1
            pb_, rpb_ = scrP[i], R_sP[i]
            act(pb_[0:nk, 0:W], sbk[0:nk, 0:W], AF.Exp, [rsb], [rpb_])
            ebv = eb[0:nk, kind * 2 + g, :].rearrange("p (a q) -> p a q", q=64)[:, :, 0:nq]
            pv = pb_[0:nk, 0:W].rearrange("p (a q) -> p a q", q=nq)
            if masked:
                stt(pv, pv, coremask[0:nk, 0:1], ebv, ALU.mult, ALU.mult, [rpb_, R_const, R_setup], [rpb_])
            else:
                tt(pv, pv, ebv, ALU.mult, [rpb_, R_setup], [rpb_])
            plist.append((pb_, rpb_, vap, nk, rlist))
        return (l, g, qc0, nq, plist)

    denb = [sb("denb%d" % i, [128, 512], F32) for i in range(3)]
    R_den = [Res("den%d" % i) for i in range(3)]

    def attn_PV(state):
        l, g, qc0, nq, plist = state
        W = 8 * nq
        H = 4 * nq
        od = acnt["OD"] % 3
        acnt["OD"] += 1
        ob, rob = psum[2 + 2 * od], R_ps[2 + 2 * od]
        db, rdb = psum[3 + 2 * od], R_ps[3 + 2 * od]
        npz = len(plist)
        for pi, (pb_, rpb_, vap, nk, rlist) in enumerate(plist):
            kk_ = 128 if nk == 64 else nk
            mm(ob[:, 0:W], vap, pb_[0:kk_, 0:W], pi == 0, pi == npz - 1, rlist + [rpb_], [rob])
            mm(db[:, 0:H], sel[0:kk_, 0, :], pb_[0:kk_, 0:H], pi == 0, False, [rpb_, R_const], [rdb])
            mm(db[:, 0:H], sel[0:kk_, 1, :], pb_[0:kk_, H:W], False, pi == npz - 1, [rpb_, R_const], [rdb])
        den, rden = denb[od], R_den[od]
        esv = es2[l][:, g, :].unsqueeze(2).to_broadcast([128, 4, nq])
        tt(den[:, 0:H].rearrange("p (a q) -> p a q", q=nq), db[:, 0:H].rearrange("p (a q) -> p a q", q=nq), esv,
           ALU.add, [rdb, R_modA[l]], [rden])
        return (l, g, qc0, nq, ob, rob, den, rden)

    def attn_norm(st2):
        l, g, qc0, nq, ob, rob, den, rden = st2
        W = 8 * nq
        H = 4 * nq
        act(den[:, 0:H], den[:, 0:H], AF.Ln, [rden], [rden])
        act(den[:, 0:H], den[:, 0:H], AF.Exp, [rden], [rden], scale=-1.0)
        for e in range(2):
            rows = slice(e * 64, (e + 1) * 64)
            cols = slice(e * 4 * nq, (e + 1) * 4 * nq)
            tt(yaT[rows, 4 * g:4 * g + 4, qc0:qc0 + nq],
               ob[rows, cols].rearrange("p (i q) -> p i q", i=4),
               den[rows, 0:H].rearrange("p (i q) -> p i q", i=4), ALU.mult,
               [rob, rden], [R_ya[4 * g + i] for i in range(4)])

    def gPS():
        return gP()

    def pstt(out, in0, scalar, in1, op0, op1, reads, writes):
        P.op("pool", lambda e, o=out, a=in0, s_=scalar, b=in1, p0=op0, p1=op1:
             e.scalar_tensor_tensor(out=o, in0=a, scalar=s_, in1=b, op0=p0, op1=p1), reads, writes)

    def pool_mixer(l, buf, rbuf, n, c0_out, first_real, stage):
        for gi, w in enumerate((2, 4, 8, 16)):
            u = buf[:, gi, :]
            if stage == "A":
                cur, rcur = u, rbuf[gi]
                lo = 0
                step = 1
                while step < w:
                    nb, rnb = gF()
                    lo2 = lo + step
                    tt(nb[:, lo2:16 + n], cur[:, lo2:16 + n], cur[:, lo2 - step:16 + n - step], ALU.add, [rcur], [rnb])
                    cur, rcur, lo = nb, rnb, lo2
                    step *= 2
                stt(dT[:, gi, c0_out:c0_out + n], cur[:, 16:16 + n], 1.0 / w, u[:, 16:16 + n], ALU.mult, ALU.subtract,
                    [rcur, rbuf[gi]], [R_d[gi]])
                if first_real:
                    tb, rt = gF()
                    tt(tb[:, 0:16], cur[:, 16:32], poolinv[:, gi, :], ALU.mult, [rcur, R_const], [rt])
                    tt(dT[:, gi, c0_out:c0_out + 16], tb[:, 0:16], u[:, 16:32], ALU.subtract, [rt, rbuf[gi]], [R_d[gi]])
            else:
                pb, rpb = gP()
                mm(pb[:, 0:n], poolw[l][:, gi, :], dT[:, gi, c0_out:c0_out + n], True, True, [R_setup, R_d[gi]], [rpb])
                act(ybT[:, gi, c0_out:c0_out + n], pb[:, 0:n], AF.Copy, [rpb, R_const], [R_yb[gi]],
                    scale=sm[l][:, S_PSC + gi:S_PSC + gi + 1])

    def layer(t, l, Tt, segs, skip_norm=False, after_q=None, before_down=None):
        last = (t == NTILES - 1)
        pseg = segs[0]
        sseg = segs[1] if len(segs) > 1 else None
        nP = pseg.n
        nch = nP // 64
        if not skip_norm:
            norm_phase(Tt, segs, s1[l], 0, l)
        for gq in range(2):
            wb, rw = w_next()
            wv = wb[:].rearrange("p (k c) -> p k c", k=8)
            for f4 in range(4):
                f = gq * 4 + f4
                pb, rpb = proj_fm(wv, rw, f4 * 128, Tt)
                act(qT[:, f, 0:Tt], pb[:, 0:Tt], AF.Copy, [rpb], [R_q[f]], scale=0.125)
            w_done()
            if gq == 0 and after_q is not None:
                after_q()
        cpy("act", KKb[:, :, 0:128], KKst[l][:], [R_KKst[l]], [R_KKb])
        cpy("act", V64b[:, 0:2, :], V64st[l][:], [R_Vst[l]], [R_Vb])
        cpy("act", puTb[:, :, 0:16], pust[l][:], [R_pust[l]], R_pub)
        wb, rw = w_next()
        wv = wb[:].rearrange("p (k c) -> p k c", k=8)
        for ge in range(4):
            pb, rpb = proj_fm(wv, rw, ge * 128, Tt)
            cpy("dve", KKb[:, ge, 128:128 + nP], pb[:, 0:nP], [rpb], [R_KKb])
            if sseg is not None:
                cpy("dve", KKs[:, ge, :], pb[:, sseg.c0:sseg.c0 + 32], [rpb], [R_KKs])
        if last or sseg is not None:
            def krhs(k, wv=wv):
                return wv[:, k, :].rearrange("p (a c) -> p a c", c=256)[:, :, 0:64]
            if last:
                pb, rpb = tok_rows(nP - 128, 128, krhs, 128, rw)
                tb, rt = gF()
                cpy("dve", tb[:, 0:128], pb[:, 0:128], [rpb], [rt])
                dma("sp", so, okp_d[l], tb[:, 0:128], [rt], (), final=True)
            if sseg is not None:
                pb, rpb = tok_rows(sseg.c0, 32, krhs, 128, rw)
                tb, rt = gF()
                cpy("dve", tb[0:32, 0:128], pb[0:32, 0:128], [rpb], [rt])
                dma("sp", so, oks_d[l, 96:128, :], tb[0:32, 0:128], [rt], (), final=True)
        w_done()
        wb, rw = w_next()
        wv = wb[:].rearrange("p (k c) -> p k c", k=8)
        for c2 in range(0, nch, 2):
            pb, rpb = gP()
            for cc in range(2):
                c = c2 + cc
                for k in range(KT):
                    mm(pb[0:64, cc * 256:(cc + 1) * 256], hT[:, k, c * 64:(c + 1) * 64], wv[:, k, 0:256],
                       k == 0, k == KT - 1, [rw, R_h[k]], [rpb])
            cpy("dve", V64b[0:64, 2 + c2:4 + c2, :], pb[0:64, :].rearrange("p (a c) -> p a c", a=2), [rpb], [R_Vb])
        if sseg is not None:
            pb, rpb = gP()
            for k in range(KT):
                mm(pb[0:32, 0:256], hT[:, k, sseg.c0:sseg.c0 + 32], wv[:, k, 0:256], k == 0, k == KT - 1,
                   [rw, R_h[k]], [rpb])
            cpy("dve", V64s[0:32, :], pb[0:32, 0:256], [rpb], [R_Vs])
        if last or sseg is not None:
            def vrhs(k, wv=wv):
                return wv[:, k, 0:256].rearrange("p (a c) -> p a c", c=128)[:, :, 0:64]
            if last:
                pb, rpb = tok_rows(nP - 128, 128, vrhs, 128, rw)
                tb, rt = gF()
                cpy("dve", tb[:, 0:128], pb[:, 0:128], [rpb], [rt])
                dma("sp", so, ovp_d[l], tb[:, 0:128], [rt], (), final=True)
            if sseg is not None:
                pb, rpb = tok_rows(sseg.c0, 32, vrhs, 128, rw)
                tb, rt = gF()
                cpy("dve", tb[0:32, 0:128], pb[0:32, 0:128], [rpb], [rt])
                dma("sp", so, ovs_d[l, 96:128, :], tb[0:32, 0:128], [rt], (), final=True)
        w_done()
        wb, rw = w_next()
        wv = wb[:].rearrange("p (k c) -> p k c", k=8)
        for g4 in range(4):
            pb, rpb = proj_fm(wv, rw, g4 * 128, Tt)
            cpy("act", puTb[:, g4, 16:16 + nP], pb[:, 0:nP], [rpb], [R_pub[g4]])
            if sseg is not None:
                cpy("act", puTs[l][:, g4, 16:48], pb[:, sseg.c0:sseg.c0 + 32], [rpb], [R_pus[l][g4]])
        if last or sseg is not None:
            def prhs(k, wv=wv):
                return wv[:, k, :]
            for (cond, c0r, dst) in ((last, nP - 32, opoolp_d), (sseg is not None, HALO, opools_d)):
                if not cond:
                    continue
                pb, rpb = tok_rows(c0r, 32, prhs, 512, rw)
                tb, rt = gF()
                cpy("dve", tb[0:32, 0:512], pb[0:32, 0:512], [rpb], [rt])
                dma("sp", so, dst[l], tb[17:32, 0:512], [rt], (), final=True)
        w_done()
        wb, rw = w_next()
        wv = wb[:].rearrange("p (k c) -> p k c", k=8)
        for g4 in range(4):
            pb, rpb = proj_fm(wv, rw, g4 * 128, Tt)
            act(uuT[:, g4, 0:Tt], pb[:, 0:Tt], AF.Gelu_apprx_tanh, [rpb], [R_uu[g4]])
        w_done()
        wb, rw = w_next()
        wv = wb[:].rearrange("p (k c) -> p k c", k=8)
        blocks = [(b * 128, 128) for b in range(nP // 128)]
        if sseg is not None:
            blocks.append((sseg.c0, 32))
        gvs = []
        memset("dve", ss4[:], 0.0, [R_ss4])
        for bi, (bc0, bn) in enumerate(blocks):
            pb, rpb = tok_rows(bc0, bn, lambda k, wv=wv: wv[:, k, :], 512, rw)
            gv, rgv = gF()
            act(gv[0:bn, 0:512], pb[0:bn, 0:512], AF.Gelu_apprx_tanh, [rpb], [rgv])
            jb, rj = denb[1], R_den[1]
            act(jb[0:bn, 0:512], gv[0:bn, 0:512], AF.Square, [rgv, R_ss4], [rj, R_ss4], accum_out=ss4[0:bn, bi:bi + 1])
            gvs.append((gv, rgv, bn))
        act(rs4[:, 0:4], ss4[:, 0:4], AF.Ln, [R_ss4, R_const], [R_rs4], bias=epsb[:, 0:1], scale=1.0 / 512)
        act(rs4[:, 0:4], rs4[:, 0:4], AF.Exp, [R_rs4], [R_rs4], scale=-0.5)
        sng = sm[l][:, S_SNG:S_SNG + 512]
        for bi, (gv, rgv, bn) in enumerate(gvs):
            stt(vn[0:bn, bi, :], gv[0:bn, 0:512], rs4[0:bn, bi:bi + 1], sng[0:bn, :], ALU.mult, ALU.mult,
                [rgv, R_rs4, R_const], [R_vn[bi]])
            if sseg is not None and bi == len(gvs) - 1:
                tb, rt = gF()
                stt(tb[0:32, 0:512], gv[0:32, 0:512], rs4[0:32, bi:bi + 1], sng[0:32, :], ALU.mult, ALU.mult,
                    [rgv, R_rs4, R_const], [rt])
                dma("sp", so, osgu_d[l], tb[0:32, 0:512], [rt], (), final=True)
        w_done()
        sgb = sm[l][:, S_SGB:S_SGB + 512].rearrange("p (g i) -> p g i", g=4)
        nblk = nP // 128
        sgu_banks = {}

        def mixers_1():
            pool_mixer(l, puTb, R_pub, nP, 0, t == 1, "A")
            if sseg is not None:
                pool_mixer(l, puTs[l], R_pus[l], 32, sseg.c0, False, "A")

        def mixers_2():
            pool_mixer(l, puTb, R_pub, nP, 0, t == 1, "B")
            if sseg is not None:
                pool_mixer(l, puTs[l], R_pus[l], 32, sseg.c0, False, "B")
            for g4 in range(4):
                if t == 0:
                    tsc(pust[l][:, g4, :], puTb[:, g4, nP:nP + 16], coremask[:, 0:1], None, ALU.mult, None,
                        [R_pub[g4], R_const], [R_pust[l]])
                else:
                    cpy("act", pust[l][:, g4, :], puTb[:, g4, nP:nP + 16], [R_pub[g4]], [R_pust[l]])
            for g4 in range(4):
                pb, rpb = gP()
                for b in range(nblk):
                    mm(pb[:, b * 128:(b + 1) * 128], vn[:, b, g4 * 128:(g4 + 1) * 128], sguw[l][:, g4, :], True, True,
                       [R_vn[b], R_setup], [rpb])
                if sseg is not None:
                    mm(pb[:, nP:nP + 32], vn[0:32, nblk, g4 * 128:(g4 + 1) * 128], sguw[l][0:32, g4, 0:32], True, True,
                       [R_vn[nblk], R_setup], [rpb])
                sgu_banks[g4] = (pb, rpb)

        def mixers_3():
            for g4 in range(4):
                pb, rpb = sgu_banks[g4]
                tb, rt = gF()
                tt(tb[:, 0:nP].rearrange("p (b i) -> p b i", i=128), pb[:, 0:nP].rearrange("p (b i) -> p b i", i=128),
                   sgb[:, g4, :].unsqueeze(1).to_broadcast([128, nblk, 128]), ALU.add, [rpb, R_const], [rt])
                if sseg is not None:
                    tt(tb[:, nP:nP + 32], pb[:, nP:nP + 32], sgb[:, g4, 0:32], ALU.add, [rpb, R_const], [rt])
                tt(uuT[:, g4, 0:Tt], tb[:, 0:Tt], uuT[:, g4, 0:Tt], ALU.mult, [rt, R_uu[g4]], [R_uu[g4]])

        for gg in range(6):
            wb, rw = w_next()
            wv = wb[:].rearrange("p (k c) -> p k c", k=8)
            for f4 in range(4):
                f = gg * 4 + f4
                pb, rpb = proj_fm(wv, rw, f4 * 128, Tt)
                act(G[:, f, 0:Tt], pb[:, 0:Tt], AF.Tanh, [rpb], [R_G[f]], scale=0.5)
            w_done()
            if gg == 0:
                mixers_1()
            elif gg == 1:
                mixers_2()
            elif gg == 2:
                mixers_3()
        its = []
        for c in range(nch):
            for g in range(2):
                pieces = []
                for pi in range(3):
                    kc0 = 64 * (c + pi)
                    masked = (t == 1 and (c + pi) < 2)
                    pieces.append((KKb[:, 2 * g, kc0:kc0 + 64], KKb[:, 2 * g + 1, kc0:kc0 + 64],
                                   V64b[:, c + pi, g * 128:(g + 1) * 128], 64, pi, [R_KKb, R_Vb], masked))
                its.append((g, c * 64, 64, pieces))
        if sseg is not None:
            for g in range(2):
                pieces = []
                for pi in range(2):
                    pieces.append((KKc[l][:, 2 * g, pi * 64:(pi + 1) * 64], KKc[l][:, 2 * g + 1, pi * 64:(pi + 1) * 64],
                                   Vc[l][:, pi, g * 128:(g + 1) * 128], 64, pi, [R_setup], False))
                pieces.append((KKs[:, 2 * g, :], KKs[:, 2 * g + 1, :], V64s[0:32, g * 128:(g + 1) * 128], 32, 2,
                               [R_KKs, R_Vs], False))
                its.append((g, sseg.c0, 32, pieces))
        st1 = {}
        st2 = {}
        nit = len(its)
        for r in range(-1, nit):
            if r + 1 < nit:
                g, qc0, nq, pieces = its[r + 1]
                st1[r + 1] = attn_S(l, g, qc0, nq, pieces)
            if 0 <= r < nit:
                st2[r] = attn_PV(st1.pop(r))
            if r - 1 >= 0:
                attn_norm(st2.pop(r - 1))
        attn_norm(st2.pop(nit - 1))
        assert not st1 and not st2
        cpy("act", KKst[l][:], KKb[:, :, nP:nP + 128], [R_KKb], [R_KKst[l]])
        cpy("act", V64st[l][:], V64b[:, nch:nch + 2, :], [R_Vb], [R_Vst[l]])
        for m in range(4):
            wb, rw = w_next()
            wa = wb[:, 0:2048].rearrange("p (k c) -> p k c", k=8)
            wp = wb[:, 2048:3072].rearrange("p (k c) -> p k c", k=4)
            wsg = wb[:, 3072:4096].rearrange("p (k c) -> p k c", k=4)
            for d2 in range(2):
                dt_ = 2 * m + d2
                cs = slice(d2 * 128, (d2 + 1) * 128)
                pa, rpa = gP()
                for k in range(KT):
                    mm(pa[:, 0:Tt], wa[:, k, cs], yaT[:, k, 0:Tt], k == 0, k == KT - 1, [rw, R_ya[k]], [rpa])
                pp, rpp = gP()
                for k in range(4):
                    mm(pp[:, 0:Tt], wp[:, k, cs], ybT[:, k, 0:Tt], k == 0, k == 3, [rw, R_yb[k]], [rpp])
                pg, rpg = gP()
                for k in range(4):
                    mm(pg[:, 0:Tt], wsg[:, k, cs], uuT[:, k, 0:Tt], k == 0, k == 3, [rw, R_uu[k]], [rpg])
                t1, r1 = gF()
                stt(t1[:, 0:Tt], G[:, dt_, 0:Tt], 1.0, pa[:, 0:Tt], ALU.add, ALU.mult, [R_G[dt_], rpa], [r1])
                t2, r2 = gF()
                stt(t2[:, 0:Tt], G[:, 8 + dt_, 0:Tt], 1.0, pp[:, 0:Tt], ALU.add, ALU.mult, [R_G[8 + dt_], rpp], [r2])
                tt(t1[:, 0:Tt], t1[:, 0:Tt], t2[:, 0:Tt], ALU.add, [r1, r2], [r1])
                t3, r3 = gF()
                stt(t3[:, 0:Tt], G[:, 16 + dt_, 0:Tt], 1.0, pg[:, 0:Tt], ALU.add, ALU.mult, [R_G[16 + dt_], rpg], [r3])
                tt(qT[:, dt_, 0:Tt], t1[:, 0:Tt], t3[:, 0:Tt], ALU.add, [r1, r3], [R_q[dt_]])
            w_done()
        for gq in range(2):
            wb, rw = w_next()
            wv = wb[:].rearrange("p (k c) -> p k c", k=8)
            for f4 in range(4):
                dt_ = gq * 4 + f4
                pb, rpb = gP()
                for k in range(KT):
                    mm(pb[:, 0:Tt], wv[:, k, f4 * 128:(f4 + 1) * 128], qT[:, k, 0:Tt], k == 0, k == KT - 1,
                       [rw, R_q[k]], [rpb])
                for sg in segs:
                    cs = slice(sg.c0, sg.c0 + sg.n)
                    stt(xT[:, dt_, cs], pb[:, cs], g1h[l][:, dt_, sg.r:sg.r + 1], xT[:, dt_, cs], ALU.mult, ALU.add,
                        [rpb, R_mod[l], R_x[dt_]], [R_x[dt_]])
            w_done()
        norm_phase(Tt, segs, s2[l], 24, l)
        cw = sm[l][:, S_CW:S_CW + 66].rearrange("p (f j) -> p f j", j=3)
        cb = sm[l][:, S_CB:S_CB + 22]
        pipe = (len(segs) == 1)
        stY = {}
        stZ = {}

        def ffn_X(f, pa, rpa, pbk, rpbk):
            lst = []
            for sg in segs:
                ab, rab = gF()
                n = sg.n
                if sg.kind == "P":
                    cpy("act", ab[:, 0:2], aprev[l][:, f, :], [R_ap[l]], [rab])
                else:
                    cpy("act", ab[:, 0:2], aprev_s[l][:, f, :], [R_aps[l]], [rab])
                cpy("act", ab[:, 2:2 + n], pa[:, sg.c0:sg.c0 + n], [rpa], [rab])
                if sg.kind == "P":
                    if t == 0:
                        tsc(aprev[l][:, f, :], ab[:, n:n + 2], coremask[:, 0:1], None, ALU.mult, None,
                            [rab, R_const], [R_ap[l]])
                    else:
                        cpy("dve", aprev[l][:, f, :], ab[:, n:n + 2], [rab], [R_ap[l]])
                acc, racc = gF()
                act(acc[:, 0:n], ab[:, 0:n], AF.Copy, [rab, R_const], [racc], scale=cw[:, f, 0:1])
                stt(acc[:, 0:n], ab[:, 1:1 + n], cw[:, f, 1:2], acc[:, 0:n], ALU.mult, ALU.add, [rab, racc, R_const], [racc])
                stt(acc[:, 0:n], ab[:, 2:2 + n], cw[:, f, 2:3], acc[:, 0:n], ALU.mult, ALU.add, [rab, racc, R_const], [racc])
                lst.append((sg, acc, racc))
            stY[f] = (lst, pbk, rpbk)

        def ffn_Y(f):
            lst, pbk, rpbk = stY.pop(f)
            for (sg, acc, racc) in lst:
                act(acc[:, 0:sg.n], acc[:, 0:sg.n], AF.Gelu_apprx_tanh, [racc, R_const], [racc], bias=cb[:, f:f + 1])
            stZ[f] = (lst, pbk, rpbk)

        def ffn_Z(f):
            lst, pbk, rpbk = stZ.pop(f)
            for (sg, acc, racc) in lst:
                tt(G[:, f, sg.c0:sg.c0 + sg.n], acc[:, 0:sg.n], pbk[:, sg.c0:sg.c0 + sg.n], ALU.mult, [racc, rpbk], [R_G[f]])

        dcnt = [0]
        for j in range(11):
            wb, rw = w_next()
            wv = wb[:].rearrange("p (k c) -> p k c", k=8)
            for fi in range(2):
                f = 2 * j + fi
                pa, rpa = proj_fm(wv, rw, fi * 128, Tt)
                pbk, rpbk = proj_fm(wv, rw, 256 + fi * 128, Tt)
                ffn_X(f, pa, rpa, pbk, rpbk)
                if pipe:
                    if f - 1 >= 0:
                        ffn_Y(f - 1)
                    if f - 2 >= 0:
                        ffn_Z(f - 2)
                else:
                    ffn_Y(f)
                    ffn_Z(f)
            if last or sseg is not None:
                for (cond, c0r, dst) in ((last, nP - 32, oconvp_d), (sseg is not None, HALO, oconvs_d)):
                    if not cond:
                        continue
                    pb, rpb = tok_rows(c0r, 32, lambda k, wv=wv: wv[:, k, 0:256], 256, rw)
                    di = dcnt[0] % 3
                    dcnt[0] += 1
                    tb, rt = denb[di], R_den[di]
                    cpy("dve", tb[0:32, 0:256], pb[0:32, 0:256], [rpb], [rt])
                    dma("sp", so, dst[l, :, 256 * j:256 * (j + 1)], tb[30:32, 0:256], [rt], (), final=True)
            w_done()
        if pipe:
            ffn_Y(FT - 1)
            ffn_Z(FT - 2)
            ffn_Z(FT - 1)
        assert not stY and not stZ
        if before_down is not None:
            before_down()
        for m in range(6):
            wb, rw = w_next()
            wv = wb[:].rearrange("p (k c) -> p k c", k=4)
            nf = 4 if m < 5 else 2
            for dt_ in range(KT):
                for fi in range(nf):
                    f = 4 * m + fi
                    mm(psum[dt_][:, 0:Tt], wv[:, fi, dt_ * 128:(dt_ + 1) * 128], G[:, f, 0:Tt], f == 0, f == FT - 1,
                       [rw, R_G[f]], [R_ps[dt_]])
            w_done()
        for dt_ in range(KT):
            for sg in segs:
                cs = slice(sg.c0, sg.c0 + sg.n)
                stt(xT[:, dt_, cs], psum[dt_][:, cs], modT[l][:, 40 + dt_, sg.r:sg.r + 1], xT[:, dt_, cs], ALU.mult, ALU.add,
                    [R_ps[dt_], R_mod[l], R_x[dt_]], [R_x[dt_]])

    def final_phase(t, Tt, segs):
        act(tlj[:, 0:1], epsb[:, 0:1], AF.Ln, [R_const], [R_tlj])
        for k in range(KT):
            act(x2[:, k, 0:Tt], xT[:, k, 0:Tt], AF.Square, [R_x[k]], [R_x2[k]])
        pb, rpb = gP()
        for k in range(KT):
            mm(pb[:, 0:Tt], ones_bf[:], x2[:, k, 0:Tt], k == 0, k == KT - 1, [R_x2[k], R_const], [rpb])
        rb, rr = gF()
        act(rb[:, 0:Tt], pb[:, 0:Tt], AF.Ln, [rpb, R_const], [rr], bias=epsb[:, 0:1], scale=1.0 / D)
        act(rb[:, 0:Tt], rb[:, 0:Tt], AF.Exp, [rr], [rr], scale=-0.5)
        for k in range(KT):
            stt(xT[:, k, 0:Tt], xT[:, k, 0:Tt], fng[:, k:k + 1], rb[:, 0:Tt], ALU.mult, ALU.mult,
                [R_x[k], rr, R_const], [R_x[k]])
        if t == 0:
            dma("sp", so, ysT_d.rearrange("(k p) t -> p k t", p=128), xT[:, :, HALO:HALO + 32], R_x, (), final=True)
        else:
            c0 = (t - 1) * 512
            dma("sp", so, yT_d.rearrange("(k p) t -> p k t", p=128)[:, :, c0:c0 + 512], xT[:, :, 0:512], R_x, (), final=True)

    XT = [xTa, xTb]
    RX = [R_xa, R_xb]
    xview = xTp.rearrange("(k p) t -> p k t", p=128)

    def load_x(t):
        buf, rbuf = XT[t % 2], RX[t % 2]
        if t == 0:
            dma("sp", xslot, buf[:, :, 0:HALO], xview[:, :, 0:HALO], (), rbuf)
            dma("sp", xslot, buf[:, :, HALO:HALO + 32], xTs.rearrange("(k p) t -> p k t", p=128), (), rbuf)
        else:
            c0 = HALO + (t - 1) * 512
            dma("sp", xslot, buf[:, :, 0:512], xview[:, :, c0:c0 + 512], (), rbuf)

    def with_x(t, fn):
        nonlocal xT, R_x
        sx, sr = xT, R_x
        xT, R_x = XT[t % 2], RX[t % 2]
        try:
            fn()
        finally:
            xT, R_x = sx, sr

    def tsz(t):
        sg_ = tile_segs(t)
        return sum(s_.n for s_ in sg_), sg_

    load_x(0)
    Tt0, segs0 = tsz(0)
    with_x(0, lambda: norm_phase(Tt0, segs0, s1[0], 0, 0))
    for t in range(NTILES):
        Tt, segs = tsz(t)
        aq = None
        if t >= 1:
            Tp, sp_ = tsz(t - 1)
            aq = (lambda tp=t - 1, Tp=Tp, sp_=sp_: with_x(tp, lambda: final_phase(tp, Tp, sp_)))
        ilv["active"] = (t == 0)
        with_x(t, lambda: layer(t, 0, Tt, segs, skip_norm=True, after_q=aq))
        ilv["active"] = False
        bd = None
        if t + 1 < NTILES:
            load_x(t + 1)
            Tn, sn_ = tsz(t + 1)
            bd = (lambda tn=t + 1, Tn=Tn, sn_=sn_: with_x(tn, lambda: norm_phase(Tn, sn_, s1[0], 0, 0)))
        with_x(t, lambda: layer(t, 1, Tt, segs, before_down=bd))
    TL, sL = tsz(NTILES - 1)
    with_x(NTILES - 1, lambda: final_phase(NTILES - 1, TL, sL))

    assert wstate["used"] == len(sched), (wstate, len(sched))
    P.emit(nc, st)
    st.close()
    return nc


def _km(W, k, ncols):
    c = W.shape[1]
    out = np.zeros((128, k, ncols), np.float32)
    out[:, :, :c] = W.reshape(k, 128, c).transpose(1, 0, 2)
    return out.reshape(128, k * ncols)


def _weights(inp):
    wl = np.zeros((2, NG_L, 128, 4096), np.float32)
    wada = np.zeros((2, 12, 128, 4096), np.float32)
    for l in range(2):
        w_in = np.asarray(inp["w_in"][l], np.float32)
        for g in range(12):
            wada[l, g] = _km(np.asarray(inp["w_ada"][l][:, g * 512:(g + 1) * 512], np.float32), 8, 512)
        gi = 0
        for gq in range(2):
            wl[l, gi] = _km(w_in[:, gq * 512:(gq + 1) * 512], 8, 512)
            gi += 1
        wk = w_in[:, 1024:1152]
        kz = np.zeros((1024, 512), np.float32)
        for g in range(2):
            kz[:, (2 * g) * 128:(2 * g) * 128 + 64] = wk[:, g * 64:(g + 1) * 64]
            kz[:, (2 * g + 1) * 128 + 64:(2 * g + 1) * 128 + 128] = wk[:, g * 64:(g + 1) * 64]
        wl[l, gi] = _km(kz, 8, 512)
        gi += 1
        wv = w_in[:, 1152:1280]
        vd = np.zeros((1024, 256), np.float32)
        for g in range(2):
            vd[:, g * 128:g * 128 + 64] = wv[:, g * 64:(g + 1) * 64]
            vd[:, g * 128 + 64:g * 128 + 128] = wv[:, g * 64:(g + 1) * 64]
        wl[l, gi] = _km(vd, 8, 512)
        gi += 1
        for c0 in (1280, 1792, 2304):
            wl[l, gi] = _km(w_in[:, c0:c0 + 512], 8, 512)
            gi += 1
        for gg in range(6):
            c0 = 2816 + gg * 512
            wl[l, gi] = _km(w_in[:, c0:c0 + 512], 8, 512)
            gi += 1
        woa = np.asarray(inp["w_o_attn"][l], np.float32)
        wop = np.asarray(inp["w_o_pool"][l], np.float32)
        wos = np.asarray(inp["w_o_sgu"][l], np.float32)
        for m in range(4):
            cs = slice(m * 256, (m + 1) * 256)
            wl[l, gi, :, 0:2048] = _km(woa[:, cs], 8, 256)
            wl[l, gi, :, 2048:3072] = _km(wop[:, cs], 4, 256)
            wl[l, gi, :, 3072:4096] = _km(wos[:, cs], 4, 256)
            gi += 1
        wout = np.asarray(inp["w_out"][l], np.float32)
        for gq in range(2):
            wl[l, gi] = _km(wout[:, gq * 512:(gq + 1) * 512], 8, 512)
            gi += 1
        wup = np.asarray(inp["ffn_w_up"][l], np.float32)
        for j in range(11):
            blk = np.concatenate([wup[:, 256 * j:256 * (j + 1)], wup[:, DFF + 256 * j:DFF + 256 * (j + 1)]], axis=1)
            wl[l, gi] = _km(blk, 8, 512)
            gi += 1
        wdn = np.asarray(inp["ffn_w_down"][l], np.float32)
        for m in range(6):
            nf = 4 if m < 5 else 2
            wl[l, gi, :, 0:nf * 1024] = _km(wdn[m * 512:m * 512 + nf * 128, :], nf, 1024)
            gi += 1
        assert gi == NG_L
    return wl, wada


def _pk(v, k):
    return np.asarray(v, np.float32).reshape(k, 128).T


def _smalls(inp):
    sm = np.zeros((2, 128, NS), np.float32)
    for l in range(2):
        sm[l, :, S_N1G:S_N1G + 8] = _pk(inp["norm1_g"][l], 8)
        sm[l, :, S_N2G:S_N2G + 8] = _pk(inp["norm2_g"][l], 8)
        sm[l, :, S_PSC:S_PSC + 4] = _pk(inp["pool_scale"][l], 4)
        cw = np.asarray(inp["ffn_conv_w"][l], np.float32)
        sm[l, :, S_CW:S_CW + 66] = cw.reshape(3, FT, 128).transpose(2, 1, 0).reshape(128, 66)
        sm[l, :, S_CB:S_CB + 22] = _pk(inp["ffn_conv_b"][l], FT)
        sm[l, :, S_BADA:S_BADA + 48] = _pk(inp["b_ada"][l], 48)
        sink = np.asarray(inp["attn_sink"][l], np.float32)
        order = [8 * g + 2 * i + e for g in range(2) for e in range(2) for i in range(4)]
        sm[l, :, S_SINK:S_SINK + 16] = sink[order][None, :]
        sm[l, :, S_SNG:S_SNG + 512] = np.asarray(inp["sgu_norm_g"][l], np.float32)[None, :]
        sm[l, :, S_SGB:S_SGB + 512] = np.asarray(inp["sgu_b"][l], np.float32).reshape(1, 512)
    return sm


def _consts():
    h = np.arange(1, 17, dtype=np.float64)
    slopes = np.exp2(-8.0 * h / 16)
    i = np.arange(64)[:, None]
    j = np.arange(64)[None, :]
    eb = np.zeros((64, 6, 2, 4, 64), np.float64)
    for kind, dist in enumerate((j + 128 - i, j + 64 - i, np.abs(j - i))):
        for g in range(2):
            for e in range(2):
                for ii in range(4):
                    hh = 8 * g + 2 * ii + e
                    eb[:, kind * 2 + g, e, ii, :] = np.exp(-slopes[hh] * dist)
    eb = eb.reshape(64, 6, 512).astype(np.float32)
    a = np.arange(128)
    sgumask = ((a[:, None] // 64) <= (a[None, :] // 64)).astype(np.float32)
    return eb, sgumask


_CACHE = {}


def kernel(**inp):
    if "nc" not in _CACHE:
        _CACHE["nc"] = build_program()
    nc = _CACHE["nc"]
    wl, wada = _weights(inp)
    smalls = _smalls(inp)
    eb, sgumask = _consts()
    fng = _pk(inp["final_norm_g"], 8)
    poolw = np.stack([np.asarray(inp["pool_w"][l], np.float32).transpose(1, 0, 2) for l in range(2)])
    sguwT = np.stack([np.asarray(inp["sgu_w"][l], np.float32).transpose(2, 0, 1) for l in range(2)])
    xp = np.asarray(inp["x_prompt"], np.float32)
    xs = np.asarray(inp["x_sample"], np.float32)
    cp = np.asarray(inp["c_prompt"], np.float32)
    cs_ = np.asarray(inp["c_sample"], np.float32)
    ck = np.asarray(inp["cache_k_win"], np.float32)
    cv = np.asarray(inp["cache_v_win"], np.float32)
    spool = np.asarray(inp["state_pool"], np.float32)
    sconv = np.asarray(inp["state_ffn_conv"], np.float32)
    in_maps = []
    for c in range(8):
        b, half = c // 2, c % 2
        S = half * HALF
        xT = np.zeros((D, NTOK), np.float32)
        lo = S - HALO
        if lo < 0:
            xT[:, HALO:] = xp[b, 0:HALF].T
        else:
            xT[:, :] = xp[b, lo:S + HALF].T
        cvec = np.stack([cp[b], cs_[c]], axis=1)
        cT = cvec.reshape(8, 128, 2).transpose(1, 0, 2)
        coremask = np.full((128, 1), float(half), np.float32)
        poolinv = np.zeros((128, 4, 16), np.float32)
        for gi, w in enumerate((2, 4, 8, 16)):
            pos = S + np.arange(16)
            poolinv[:, gi, :] = (1.0 / np.minimum(pos + 1, w))[None, :]
        kcz = np.zeros((2, 128, 4, 128), np.float32)
        vcd = np.zeros((2, 64, 2, 256), np.float32)
        for l in range(2):
            for g in range(2):
                kt_ = ck[l, c, :, g, :].T
                kcz[l, 0:64, 2 * g, :] = kt_
                kcz[l, 64:128, 2 * g + 1, :] = kt_
                for pi in range(2):
                    vv = cv[l, c, pi * 64:(pi + 1) * 64, g, :]
                    vcd[l, :, pi, g * 128:g * 128 + 64] = vv
                    vcd[l, :, pi, g * 128 + 64:g * 128 + 128] = vv
        spoolT = spool[:, c].reshape(2, 15, 4, 128).transpose(0, 3, 2, 1)
        sconvT = sconv[:, c].reshape(2, 2, FT, 128).transpose(0, 3, 2, 1)
        in_maps.append({
            "xTp": xT, "xTs": np.ascontiguousarray(xs[c].T), "cT": np.ascontiguousarray(cT),
            "wl": wl, "wada": wada, "smalls": smalls, "fng": np.ascontiguousarray(fng),
            "poolw": np.ascontiguousarray(poolw), "sguwT": np.ascontiguousarray(sguwT), "sgumask": sgumask,
            "eb": eb, "coremask": coremask, "poolinv": poolinv, "kcz": kcz, "vcd": vcd,
            "cktok": np.ascontiguousarray(ck[:, c].reshape(2, 128, 128)),
            "cvtok": np.ascontiguousarray(cv[:, c].reshape(2, 128, 128)),
            "spoolT": np.ascontiguousarray(spoolT), "sconvT": np.ascontiguousarray(sconvT),
        })
    res = run_bass_kernel_spmd(nc, in_maps, core_ids=list(range(8))).results
    y_prompt = np.zeros((4, 8192, D), np.float32)
    y_sample = np.zeros((8, 32, D), np.float32)
    kp = np.zeros((2, 4, 128, 2, 64), np.float32)
    vp = np.zeros_like(kp)
    pp = np.zeros((2, 4, 15, 512), np.float32)
    cpv = np.zeros((2, 4, 2, DFF), np.float32)
    ks = np.zeros((2, 8, 128, 2, 64), np.float32)
    vs = np.zeros_like(ks)
    ps_ = np.zeros((2, 8, 15, 512), np.float32)
    cs2 = np.zeros((2, 8, 2, DFF), np.float32)
    sg = np.zeros((2, 8, 32, 512), np.float32)
    for c in range(8):
        b, half = c // 2, c % 2
        r = res[c]
        y_prompt[b, half * HALF:(half + 1) * HALF] = r["yT"].T
        y_sample[c] = r["ysT"].T
        if half == 1:
            kp[:, b] = r["okp"].reshape(2, 128, 2, 64)
            vp[:, b] = r["ovp"].reshape(2, 128, 2, 64)
            pp[:, b] = r["opoolp"]
            cpv[:, b] = r["oconvp"]
        ks[:, c] = r["oks"].reshape(2, 128, 2, 64)
        vs[:, c] = r["ovs"].reshape(2, 128, 2, 64)
        ps_[:, c] = r["opools"]
        cs2[:, c] = r["oconvs"]
        sg[:, c] = r["osgu"]
    return (y_prompt, y_sample, kp, vp, pp, cpv, ks, vs, ps_, cs2, sg)
```

```python
import numpy as np
from contextlib import ExitStack
import concourse.bass as bass
import concourse.mybir as mybir
from concourse.bass_utils import run_bass_kernel_spmd

F32 = mybir.dt.float32
BF16 = mybir.dt.bfloat16
AF = mybir.ActivationFunctionType
ALU = mybir.AluOpType

D = 1024
KT = 8
DFF = 2816
FT = 22
NIN = 5888
EPS = 1e-6
HALO = 384
HALF = 4096
NTOK = HALO + HALF
NTILES = 9
NG_L = 36
NBUF = 4
S_N1G, S_N2G, S_PSC, S_CW, S_CB, S_BADA, S_SINK, S_SNG, S_SGB = 0, 8, 16, 20, 86, 108, 156, 172, 684
NS = 1196

ENGS = ("pe", "act", "dve", "pool", "sp")


class Res:
    __slots__ = ("name", "excl", "last_w", "readers")

    def __init__(self, name, excl=False):
        self.name = name
        self.excl = excl
        self.last_w = None
        self.readers = []


class Op:
    __slots__ = ("eng", "idx", "fn", "waits", "needs_inc", "inc_val", "dma", "dma_val")

    def __init__(self, eng, idx, fn):
        self.eng = eng
        self.idx = idx
        self.fn = fn
        self.waits = []
        self.needs_inc = False
        self.inc_val = None
        self.dma = None
        self.dma_val = None


class DmaSlot:
    def __init__(self, prog, name):
        self.name = name
        self.count = 0
        self.sem = None
        prog.slots.append(self)


class Prog:
    def __init__(self):
        self.ops = {e: [] for e in ENGS}
        self.slots = []
        self.waited = {e: {} for e in ENGS}
        self.final_dma = []

    def _deps(self, reads, writes):
        deps = []
        for r in reads:
            if r.excl:
                if r.last_w is not None:
                    deps.append(r.last_w)
                deps.extend(r.readers)
            elif r.last_w is not None:
                deps.append(r.last_w)
        for w in writes:
            if w.last_w is not None:
                deps.append(w.last_w)
            deps.extend(w.readers)
        return deps

    def _commit(self, op, reads, writes):
        for r in reads:
            if r.excl:
                r.last_w = op
                r.readers = []
            else:
                r.readers.append(op)
                if len(r.readers) > 64:
                    r.readers = r.readers[-48:]
        for w in writes:
            w.last_w = op
            w.readers = []

    def _add_waits(self, op, deps):
        need = {}
        for d in deps:
            if d is op:
                continue
            if d.dma is not None:
                key = ("dma", d.dma)
                need[key] = max(need.get(key, 0), d.dma_val)
            else:
                if d.eng == op.eng and op.dma is None:
                    if op.eng == "pe":
                        continue
                key = ("eng", d.eng)
                cur = need.get(key)
                if cur is None or d.idx > cur.idx:
                    need[key] = d
        wd = self.waited[op.eng]
        for key, v in need.items():
            if key[0] == "dma":
                if wd.get(key, 0) >= v:
                    continue
                wd[key] = v
                op.waits.append(("dma", key[1], v))
            else:
                prev = wd.get(key, -1)
                if prev >= v.idx:
                    continue
                wd[key] = v.idx
                v.needs_inc = True
                op.waits.append(("eng", v, None))

    def op(self, eng, fn, reads=(), writes=()):
        o = Op(eng, len(self.ops[eng]), fn)
        self._add_waits(o, self._deps(reads, writes))
        self._commit(o, reads, writes)
        self.ops[eng].append(o)
        return o

    def dma(self, queue, slot, fn, reads=(), writes=(), final=False):
        o = Op(queue, len(self.ops[queue]), fn)
        o.dma = slot
        if final and slot.count > 0:
            o.waits.append(("dma", slot, slot.count))
        slot.count += 16
        o.dma_val = slot.count
        self._add_waits(o, self._deps(reads, writes))
        self._commit(o, reads, writes)
        self.ops[queue].append(o)
        if final:
            self.final_dma.append(o)
        return o

    def emit(self, nc, stack):
        esem = {}
        for e in ENGS:
            esem[e] = stack.enter_context(nc.semaphore("sem_" + e))
            c = 0
            for o in self.ops[e]:
                if o.dma is None and o.needs_inc:
                    c += 1
                    o.inc_val = c
        for i, s in enumerate(self.slots):
            if s.count > 0:
                s.sem = stack.enter_context(nc.semaphore("dsem%d" % i))
        block = stack.enter_context(nc.Block())

        def run(ename):
            def body(e):
                for o in self.ops[ename]:
                    for (kind, a, v) in o.waits:
                        if kind == "dma":
                            e.wait_ge(a.sem, v)
                        else:
                            e.wait_ge(esem[a.eng], a.inc_val)
                    ins = o.fn(e)
                    if o.dma is not None:
                        ins.then_inc(o.dma.sem, 16)
                    elif o.needs_inc:
                        ins.then_inc(esem[ename], 1)
                if ename == "sp":
                    fin = {}
                    for o in self.final_dma:
                        fin[o.dma] = max(fin.get(o.dma, 0), o.dma_val)
                    for s, v in fin.items():
                        e.wait_ge(s.sem, v)
            return body

        block.tensor(run("pe"))
        block.scalar(run("act"))
        block.vector(run("dve"))
        block.gpsimd(run("pool"))
        block.sync(run("sp"))


class Seg:
    def __init__(self, kind, c0, n, r):
        self.kind, self.c0, self.n, self.r = kind, c0, n, r


def tile_segs(t):
    if t == 0:
        return [Seg("P", 0, HALO, 0), Seg("S", HALO, 32, 1)]
    return [Seg("P", 0, 512, 0)]


def build_program():
    nc = bass.Bass("TRN2", target_bir_lowering=False)
    P = Prog()
    st = ExitStack()

    def din(name, shape):
        return nc.dram_tensor(name, list(shape), F32, kind="ExternalInput").ap()

    def dout(name, shape):
        return nc.dram_tensor(name, list(shape), F32, kind="ExternalOutput").ap()

    xTp = din("xTp", [D, NTOK])
    xTs = din("xTs", [D, 32])
    cT_d = din("cT", [128, KT, 2])
    wl_d = din("wl", [2, NG_L, 128, 4096])
    wada_d = din("wada", [2, 12, 128, 4096])
    smalls_d = din("smalls", [2, 128, NS])
    fng_d = din("fng", [128, KT])
    poolw_d = din("poolw", [2, 128, 4, 128])
    sguwT_d = din("sguwT", [2, 128, 4, 128])
    sgumask_d = din("sgumask", [128, 128])
    eb_d = din("eb", [64, 6, 512])
    coremask_d = din("coremask", [128, 1])
    poolinv_d = din("poolinv", [128, 4, 16])
    kcz_d = din("kcz", [2, 128, 4, 128])
    vcd_d = din("vcd", [2, 64, 2, 256])
    cktok_d = din("cktok", [2, 128, 128])
    cvtok_d = din("cvtok", [2, 128, 128])
    spoolT_d = din("spoolT", [2, 128, 4, 15])
    sconvT_d = din("sconvT", [2, 128, FT, 2])

    yT_d = dout("yT", [D, HALF])
    ysT_d = dout("ysT", [D, 32])
    okp_d = dout("okp", [2, 128, 128])
    ovp_d = dout("ovp", [2, 128, 128])
    opoolp_d = dout("opoolp", [2, 15, 512])
    oconvp_d = dout("oconvp", [2, 2, DFF])
    oks_d = dout("oks", [2, 128, 128])
    ovs_d = dout("ovs", [2, 128, 128])
    opools_d = dout("opools", [2, 15, 512])
    oconvs_d = dout("oconvs", [2, 2, DFF])
    osgu_d = dout("osgu", [2, 32, 512])

    def sb(name, shape, dt):
        return st.enter_context(nc.sbuf_tensor("sb_" + name, list(shape), dt))

    xTa = sb("xTa", [128, KT, 512], F32)
    xTb = sb("xTb", [128, KT, 512], F32)
    xT = xTa
    hT = sb("hT", [128, KT, 512], BF16)
    qT = sb("qT", [128, KT, 512], BF16)
    yaT = sb("yaT", [128, KT, 512], BF16)
    x2 = yaT
    G = sb("G", [128, 24, 512], BF16)
    KKb = sb("KKb", [128, 4, 128 + 512], BF16)
    KKst = [sb("KKst%d" % l, [128, 4, 128], BF16) for l in range(2)]
    KKs = sb("KKs", [128, 4, 32], BF16)
    KKc = [sb("KKc%d" % l, [128, 4, 128], BF16) for l in range(2)]
    V64b = sb("V64b", [128, 10, 256], BF16)
    V64st = [sb("V64st%d" % l, [128, 2, 256], BF16) for l in range(2)]
    V64s = sb("V64s", [64, 256], BF16)
    Vc = [sb("Vc%d" % l, [128, 2, 256], BF16) for l in range(2)]
    puTb = sb("puTb", [128, 4, 16 + 512], F32)
    pust = [sb("pust%d" % l, [128, 4, 16], F32) for l in range(2)]
    puTs = [sb("puTs%d" % l, [128, 4, 16 + 32], F32) for l in range(2)]
    dT = sb("dT", [128, 4, 512], BF16)
    ybT = sb("ybT", [128, 4, 512], BF16)
    uuT = sb("uuT", [128, 4, 512], BF16)
    vn = sb("vn", [128, 4, 512], BF16)
    aprev = [sb("aprev%d" % l, [128, FT, 2], F32) for l in range(2)]
    aprev_s = [sb("aprevs%d" % l, [128, FT, 2], F32) for l in range(2)]
    NSF, NSB = 6, 0
    scrF = [sb("scrF%d" % i, [128, 528], F32) for i in range(NSF)]
    scrB = [sb("scrB%d" % i, [128, 512], BF16) for i in range(NSB)]
    wbuf = [sb("wbuf%d" % i, [128, 4096], BF16) for i in range(NBUF)]
    sm = [sb("sm%d" % l, [128, NS], F32) for l in range(2)]
    fng = sb("fng", [128, KT], F32)
    poolw = [sb("poolw%d" % l, [128, 4, 128], BF16) for l in range(2)]
    sguw = [sb("sguw%d" % l, [128, 4, 128], BF16) for l in range(2)]
    sgumask = sb("sgumask", [128, 128], F32)
    eb = sb("eb", [64, 6, 512], BF16)
    coremask = sb("coremask", [128, 1], F32)
    poolinv = sb("poolinv", [128, 4, 16], F32)
    ones_bf = sb("ones_bf", [128, 128], BF16)
    sel = sb("sel", [128, 2, 128], BF16)
    es2 = [sb("es2_%d" % l, [128, 2, 4], F32) for l in range(2)]
    cT = sb("cT", [128, KT, 2], F32)
    scT = sb("scT", [128, KT, 2], BF16)
    modT = [sb("modT%d" % l, [128, 48, 2], F32) for l in range(2)]
    s1 = [sb("s1_%d" % l, [128, KT, 2], F32) for l in range(2)]
    s2 = [sb("s2_%d" % l, [128, KT, 2], F32) for l in range(2)]
    g1h = [sb("g1h_%d" % l, [128, KT, 2], F32) for l in range(2)]
    es = [sb("es%d" % l, [128, 16], F32) for l in range(2)]
    ss4 = sb("ss4", [128, 8], F32)
    rs4 = sb("rs4", [128, 8], F32)
    psum = [st.enter_context(nc.psum_tensor("ps%d" % i, [128, 512], F32)) for i in range(8)]

    R_xa = [Res("xa%d" % k) for k in range(KT)]
    R_xb = [Res("xb%d" % k) for k in range(KT)]
    R_x = R_xa
    R_h = [Res("h%d" % k) for k in range(KT)]
    R_q = [Res("q%d" % k) for k in range(KT)]
    R_ya = [Res("ya%d" % k) for k in range(KT)]
    R_x2 = R_ya
    R_G = [Res("G%d" % k) for k in range(24)]
    R_KKb = Res("KKb")
    R_KKst = [Res("KKst%d" % l) for l in range(2)]
    R_KKs = Res("KKs")
    R_Vb = Res("Vb")
    R_Vst = [Res("Vst%d" % l) for l in range(2)]
    R_Vs = Res("Vs")
    R_pub = [Res("pub%d" % g) for g in range(4)]
    R_pust = [Res("pust%d" % l) for l in range(2)]
    R_pus = [[Res("pus%d_%d" % (l, g)) for g in range(4)] for l in range(2)]
    R_d = [Res("d%d" % g) for g in range(4)]
    R_yb = [Res("yb%d" % g) for g in range(4)]
    R_uu = [Res("uu%d" % g) for g in range(4)]
    R_vn = [Res("vn%d" % g) for g in range(4)]
    R_ap = [Res("ap%d" % l) for l in range(2)]
    R_aps = [Res("aps%d" % l) for l in range(2)]
    R_sF = [Res("sF%d" % i) for i in range(NSF)]
    R_sB = [Res("sB%d" % i) for i in range(NSB)]
    R_w = [Res("w%d" % i) for i in range(NBUF)]
    R_ps = [Res("ps%d" % i, excl=True) for i in range(8)]
    R_const = Res("const")
    R_mod = [Res("mod%d" % l) for l in range(2)]
    R_ss4 = Res("ss4")
    R_rs4 = Res("rs4")

    cnt = {"F": 0, "B": 0, "ps": 0}

    def gF():
        i = cnt["F"] % NSF
        cnt["F"] += 1
        return scrF[i], R_sF[i]

    def gB():
        i = cnt["B"] % NSB
        cnt["B"] += 1
        return scrB[i], R_sB[i]

    def gP():
        i = cnt["ps"] % 8
        cnt["ps"] += 1
        return psum[i], R_ps[i]

    def mm(out, lhsT, rhs, start, stop, reads, writes):
        P.op("pe", lambda e, o=out, l=lhsT, r=rhs, s=start, t=stop: e.matmul(o, lhsT=l, rhs=r, start=s, stop=t),
             reads, writes)

    def act(out, in_, func, reads, writes, bias=None, scale=None, accum_out=None):
        kw = {}
        if bias is not None:
            kw["bias"] = bias
        if scale is not None:
            kw["scale"] = scale
        if accum_out is not None:
            kw["accum_out"] = accum_out
        P.op("act", lambda e, o=out, i=in_, f=func, kw=kw: e.activation(out=o, in_=i, func=f, **kw), reads, writes)

    def tt(out, in0, in1, op, reads, writes, eng="dve"):
        P.op(eng, lambda e, o=out, a=in0, b=in1, p=op: e.tensor_tensor(out=o, in0=a, in1=b, op=p), reads, writes)

    def stt(out, in0, scalar, in1, op0, op1, reads, writes):
        P.op("dve", lambda e, o=out, a=in0, s=scalar, b=in1, p0=op0, p1=op1:
             e.scalar_tensor_tensor(out=o, in0=a, scalar=s, in1=b, op0=p0, op1=p1), reads, writes)

    def tsc(out, in0, s1_, s2_, op0, op1, reads, writes):
        if s2_ is None:
            P.op("dve", lambda e, o=out, a=in0, s=s1_, p0=op0: e.tensor_scalar(out=o, in0=a, scalar1=s, scalar2=None, op0=p0),
                 reads, writes)
        else:
            P.op("dve", lambda e, o=out, a=in0, s=s1_, s2v=s2_, p0=op0, p1=op1:
                 e.tensor_scalar(out=o, in0=a, scalar1=s, scalar2=s2v, op0=p0, op1=p1), reads, writes)

    def cpy(eng, out, in_, reads, writes):
        if eng == "act":
            act(out, in_, AF.Copy, reads, writes)
        else:
            P.op(eng, lambda e, o=out, i=in_: e.tensor_copy(out=o, in_=i), reads, writes)

    def memset(eng, ap, val, writes):
        P.op(eng, lambda e, a=ap, v=val: e.memset(a, v), (), writes)

    oslots = []
    ocnt = [0]

    def dma(queue, slot, out, in_, reads, writes, final=False):
        if final:
            if not oslots:
                oslots.extend(DmaSlot(P, "out%d" % i) for i in range(8))
            slot = oslots[ocnt[0] % 8]
            ocnt[0] += 1
        return P.dma(queue, slot, lambda e, o=out, i=in_: e.dma_start(out=o, in_=i), reads, writes, final=final)

    wslots = [DmaSlot(P, "w%d" % i) for i in range(NBUF)]
    sched = []
    for g in range(4):
        sched.append((wada_d[0, g], 4096))

    def layer_sched(l):
        out = []
        for g in range(NG_L):
            n = 4096
            if g == 3:
                n = -256
            if g == NG_L - 1:
                n = 2048
            out.append((wl_d[l, g], n))
        return out

    L1_SLOTS = [8, 10, 11, 13, 15, 17, 19, 21, 23, 25, 27, 29]

    def ilv_plan(gi, n1):
        if gi < 8:
            return (0, 4 + gi)
        if gi in L1_SLOTS:
            return (1, L1_SLOTS.index(gi))
        return None

    for t in range(NTILES):
        for l in range(2):
            grp = layer_sched(l)
            if t == 0 and l == 0:
                n1 = 0
                for gi, g_ in enumerate(grp):
                    sched.append(g_)
                    pl = ilv_plan(gi, n1)
                    if pl is not None:
                        sched.append((wada_d[pl[0], pl[1]], 4096))
                        if pl[0] == 1:
                            n1 += 1
                assert n1 == 12
            else:
                sched.extend(grp)
    wstate = {"issued": 0, "used": 0}

    def w_issue():
        i = wstate["issued"]
        if i >= len(sched):
            return
        src, n = sched[i]
        s = i % NBUF
        if n == -256:
            o = wbuf[s][:].rearrange("p (k c) -> p k c", k=8)[:, :, 0:256]
            src_ap = src.rearrange("p (k c) -> p k c", k=8)[:, :, 0:256]
        else:
            o = wbuf[s][:, 0:n]
            src_ap = src[:, 0:n]
        dma("pool", wslots[s], o, src_ap, (), [R_w[s]])
        wstate["issued"] += 1

    def w_next():
        i = wstate["used"]
        assert i < wstate["issued"]
        s = i % NBUF
        wstate["used"] += 1
        return wbuf[s], R_w[s]

    ilv = {"active": False, "gi": 0, "n1": 0}

    def w_done():
        w_issue()
        if ilv["active"]:
            gi = ilv["gi"]
            ilv["gi"] += 1
            pl = ilv_plan(gi, ilv["n1"])
            if pl is not None:
                mod_group(pl[0], pl[1])
                if pl[0] == 1:
                    ilv["n1"] += 1

    for _ in range(NBUF):
        w_issue()

    ld = [DmaSlot(P, "ld%d" % i) for i in range(4)]
    for l in range(2):
        dma("sp", ld[0], sm[l][:], smalls_d[l], (), [R_const])
    dma("sp", ld[0], fng[:], fng_d[:, :], (), [R_const])
    dma("sp", ld[0], sgumask[:], sgumask_d[:, :], (), [R_const])
    dma("sp", ld[0], coremask[:], coremask_d[:, :], (), [R_const])
    dma("sp", ld[0], poolinv[:], poolinv_d[:, :, :], (), [R_const])
    dma("sp", ld[0], cT[:], cT_d[:, :, :], (), [R_const])
    R_setup = Res("setup")
    for l in range(2):
        memset("dve", puTs[l][:, :, 0:1], 0.0, R_pus[l])
        dma("sp", DmaSlot(P, "stp%d" % l), puTs[l][:, :, 1:16], spoolT_d[l], (), [R_pus[l][0], R_pus[l][1], R_pus[l][2], R_pus[l][3]])
        dma("sp", DmaSlot(P, "stc%d" % l), aprev_s[l][:], sconvT_d[l], (), [R_aps[l]])
    dma("pool", ld[2], eb[:], eb_d[:, :, :], (), [R_setup])
    for l in range(2):
        dma("pool", ld[2], poolw[l][:], poolw_d[l], (), [R_setup])
        dma("pool", ld[2], KKc[l][:], kcz_d[l], (), [R_setup])
        memset("dve", Vc[l][64:128, :, :], 0.0, [R_setup])
        dma("pool", ld[2], Vc[l][0:64, :, :], vcd_d[l], (), [R_setup])
    so = None
    for l in range(2):
        dma("sp", so, oks_d[l, 0:96, :], cktok_d[l, 32:128, :], (), (), final=True)
        dma("sp", so, ovs_d[l, 0:96, :], cvtok_d[l, 32:128, :], (), (), final=True)

    memset("dve", ones_bf[:], 1.0, [R_const])
    memset("dve", sel[:], 0.0, [R_const])
    memset("dve", sel[:, 0, 0:64], 1.0, [R_const])
    memset("dve", sel[:, 1, 64:128], 1.0, [R_const])
    for l in range(2):
        memset("dve", aprev[l][:], 0.0, [R_ap[l]])
        memset("dve", pust[l][:], 0.0, [R_pust[l]])
        memset("dve", KKst[l][:], 0.0, [R_KKst[l]])
        memset("dve", V64st[l][:], 0.0, [R_Vst[l]])
    memset("dve", V64b[:], 0.0, [R_Vb])
    for l in range(2):
        sguw_f = scrF[0][:, 0:512].rearrange("p (g i) -> p g i", g=4)
        dma("sp", ld[3], sguw_f, sguwT_d[l], (), [R_sF[0]])
        for g in range(4):
            tt(sguw[l][:, g, :], sguw_f[:, g, :], sgumask[:], ALU.mult, [R_sF[0], R_const], [R_setup])
    act(scT[:], cT[:], AF.Silu, [R_const], [R_setup])
    R_modA = [Res("modA%d" % l) for l in range(2)]

    def mod_group(l, g):
        wb, rw = w_next()
        wv = wb[:].rearrange("p (k c) -> p k c", k=8)
        pb, rpb = gP()
        for f4 in range(4):
            for k in range(KT):
                mm(pb[:, 2 * f4:2 * f4 + 2], wv[:, k, f4 * 128:(f4 + 1) * 128], scT[:, k, :], k == 0, k == KT - 1,
                   [rw, R_setup], [rpb])
        w_issue()
        rm = R_modA[l] if g < 4 else R_mod[l]
        bada = sm[l][:, S_BADA + 4 * g:S_BADA + 4 * g + 4]
        tt(modT[l][:, 4 * g:4 * g + 4, :], pb[:, 0:8].rearrange("p (f r) -> p f r", r=2),
           bada.unsqueeze(2).to_broadcast([128, 4, 2]), ALU.add, [rpb, R_const], [rm])
        if g == 3:
            n1g = sm[l][:, S_N1G:S_N1G + 8].unsqueeze(2).to_broadcast([128, 8, 2])
            stt(s1[l][:], modT[l][:, 8:16, :], 1.0, n1g, ALU.add, ALU.mult, [R_modA[l], R_const], [R_modA[l]])
            act(es[l][:], sm[l][:, S_SINK:S_SINK + 16], AF.Exp, [R_const], [R_modA[l]])
            esv_ = es[l][:].rearrange("p (g e i) -> p g e i", g=2, e=2)
            cpy("dve", es2[l][0:64, :, :], esv_[0:64, :, 0, :], [R_modA[l]], [R_modA[l]])
            cpy("dve", es2[l][64:128, :, :], esv_[64:128, :, 1, :], [R_modA[l]], [R_modA[l]])
        if g == 11:
            n2g = sm[l][:, S_N2G:S_N2G + 8].unsqueeze(2).to_broadcast([128, 8, 2])
            stt(s2[l][:], modT[l][:, 32:40, :], 1.0, n2g, ALU.add, ALU.mult, [R_mod[l], R_const], [R_mod[l]])
            tsc(g1h[l][:], modT[l][:, 16:24, :], 0.5, None, ALU.mult, None, [R_mod[l]], [R_mod[l]])

    for g in range(4):
        mod_group(0, g)
    xslot = DmaSlot(P, "xin")

    def norm_phase(Tt, segs, svec, sh_base, l):
        for k in range(KT):
            act(x2[:, k, 0:Tt], xT[:, k, 0:Tt], AF.Square, [R_x[k]], [R_x2[k]])
        pb, rpb = gP()
        for k in range(KT):
            mm(pb[:, 0:Tt], ones_bf[:], x2[:, k, 0:Tt], k == 0, k == KT - 1, [R_x2[k], R_const], [rpb])
        rb, rr = rstdb, R_rstd
        act(rb[:, 0:Tt], pb[:, 0:Tt], AF.Ln, [rpb, R_const], [rr], bias=epsb[:, 0:1], scale=1.0 / D)
        act(rb[:, 0:Tt], rb[:, 0:Tt], AF.Exp, [rr], [rr], scale=-0.5)
        for k in range(KT):
            tb, rt = gF()
            for sg in segs:
                cs = slice(sg.c0, sg.c0 + sg.n)
                rmod = R_modA[l] if sh_base == 0 else R_mod[l]
                stt(tb[:, cs], xT[:, k, cs], svec[:, k, sg.r:sg.r + 1], rb[:, cs], ALU.mult, ALU.mult,
                    [R_x[k], rr, rmod], [rt])
                act(hT[:, k, cs], tb[:, cs], AF.Identity, [rt, rmod], [R_h[k]],
                    bias=modT[l][:, sh_base + k, sg.r:sg.r + 1])

    rstdb = sb("rstdb", [128, 512], F32)
    R_rstd = Res("rstd")
    epsb = sb("epsb", [128, 1], F32)
    memset("dve", epsb[:], EPS, [R_const])

    def proj_fm(wv, rw, col0, Tt):
        pb, rpb = gP()
        for k in range(KT):
            mm(pb[:, 0:Tt], wv[:, k, col0:col0 + 128], hT[:, k, 0:Tt], k == 0, k == KT - 1, [rw, R_h[k]], [rpb])
        return pb, rpb

    def tok_rows(c0, n, rhs_of_k, N, rw):
        pb, rpb = gP()
        for k in range(KT):
            mm(pb[0:n, 0:N], hT[:, k, c0:c0 + n], rhs_of_k(k), k == 0, k == KT - 1, [rw, R_h[k]], [rpb])
        return pb, rpb

    NSP = 6
    scrP = [sb("scrP%d" % i, [128, 512], BF16) for i in range(NSP)]
    R_sP = [Res("sP%d" % i) for i in range(NSP)]
    for i in range(NSP):
        memset("dve", scrP[i][:], 0.0, [R_sP[i]])
    acnt = {"P": 0, "S": 0, "OD": 0}

    def attn_S(l, g, qc0, nq, pieces):
        W = 8 * nq
        plist = []
        for pi, (kk0, kk1, vap, nk, kind, rlist, masked) in enumerate(pieces):
            bi = acnt["S"] % 2
            acnt["S"] += 1
            sbk, rsb = psum[bi], R_ps[bi]
            for e, kk in ((0, kk0), (1, kk1)):
                mm(sbk[0:nk, e * 4 * nq:(e + 1) * 4 * nq], kk, qT[:, 4 * g:4 * g + 4, qc0:qc0 + nq], True, True,
                   rlist + [R_q[4 * g + i] for i in range(4)], [rsb])
            i = acnt["P"] % NSP
            acnt["P"] += 1
            pb_, rpb_ = scrP[i], R_sP[i]
            act(pb_[0:nk, 0:W], sbk[0:nk, 0:W], AF.Exp, [rsb], [rpb_])
            ebv = eb[0:nk, kind * 2 + g, :].rearrange("p (a q) -> p a q", q=64)[:, :, 0:nq]
            pv = pb_[0:nk, 0:W].rearrange("p (a q) -> p a q", q=nq)
            if masked:
                stt(pv, pv, coremask[0:nk, 0:1], ebv, ALU.mult, ALU.mult, [rpb_, R_const, R_setup], [rpb_])
            else:
                tt(pv, pv, ebv, ALU.mult, [rpb_, R_setup], [rpb_])
            plist.append((pb_, rpb_, vap, nk, rlist))
        return (l, g, qc0, nq, plist)

    denb = [sb("denb%d" % i, [128, 512], F32) for i in range(3)]
    R_den = [Res("den%d" % i) for i in range(3)]

    def attn_PV(state):
        l, g, qc0, nq, plist = state
        W = 8 * nq
        H = 4 * nq
        od = acnt["OD"] % 3
        acnt["OD"] += 1
        ob, rob = psum[2 + 2 * od], R_ps[2 + 2 * od]
        db, rdb = psum[3 + 2 * od], R_ps[3 + 2 * od]
        npz = len(plist)
        for pi, (pb_, rpb_, vap, nk, rlist) in enumerate(plist):
            kk_ = 128 if nk == 64 else nk
            mm(ob[:, 0:W], vap, pb_[0:kk_, 0:W], pi == 0, pi == npz - 1, rlist + [rpb_], [rob])
            mm(db[:, 0:H], sel[0:kk_, 0, :], pb_[0:kk_, 0:H], pi == 0, False, [rpb_, R_const], [rdb])
            mm(db[:, 0:H], sel[0:kk_, 1, :], pb_[0:kk_, H:W], False, pi == npz - 1, [rpb_, R_const], [rdb])
        den, rden = denb[od], R_den[od]
        esv = es2[l][:, g, :].unsqueeze(2).to_broadcast([128, 4, nq])
        tt(den[:, 0:H].rearrange("p (a q) -> p a q", q=nq), db[:, 0:H].rearrange("p (a q) -> p a q", q=nq), esv,
           ALU.add, [rdb, R_modA[l]], [rden])
        return (l, g, qc0, nq, ob, rob, den, rden)

    def attn_norm(st2):
        l, g, qc0, nq, ob, rob, den, rden = st2
        W = 8 * nq
        H = 4 * nq
        act(den[:, 0:H], den[:, 0:H], AF.Ln, [rden], [rden])
        act(den[:, 0:H], den[:, 0:H], AF.Exp, [rden], [rden], scale=-1.0)
        for e in range(2):
            rows = slice(e * 64, (e + 1) * 64)
            cols = slice(e * 4 * nq, (e + 1) * 4 * nq)
            tt(yaT[rows, 4 * g:4 * g + 4, qc0:qc0 + nq],
               ob[rows, cols].rearrange("p (i q) -> p i q", i=4),
               den[rows, 0:H].rearrange("p (i q) -> p i q", i=4), ALU.mult,
               [rob, rden], [R_ya[4 * g + i] for i in range(4)])

    def gPS():
        return gP()

    def pstt(out, in0, scalar, in1, op0, op1, reads, writes):
        P.op("pool", lambda e, o=out, a=in0, s_=scalar, b=in1, p0=op0, p1=op1:
             e.scalar_tensor_tensor(out=o, in0=a, scalar=s_, in1=b, op0=p0, op1=p1), reads, writes)

    def pool_mixer(l, buf, rbuf, n, c0_out, first_real, stage):
        for gi, w in enumerate((2, 4, 8, 16)):
            u = buf[:, gi, :]
            if stage == "A":
                cur, rcur = u, rbuf[gi]
                lo = 0
                step = 1
                while step < w:
                    nb, rnb = gF()
                    lo2 = lo + step
                    tt(nb[:, lo2:16 + n], cur[:, lo2:16 + n], cur[:, lo2 - step:16 + n - step], ALU.add, [rcur], [rnb])
                    cur, rcur, lo = nb, rnb, lo2
                    step *= 2
                stt(dT[:, gi, c0_out:c0_out + n], cur[:, 16:16 + n], 1.0 / w, u[:, 16:16 + n], ALU.mult, ALU.subtract,
                    [rcur, rbuf[gi]], [R_d[gi]])
                if first_real:
                    tb, rt = gF()
                    tt(tb[:, 0:16], cur[:, 16:32], poolinv[:, gi, :], ALU.mult, [rcur, R_const], [rt])
                    tt(dT[:, gi, c0_out:c0_out + 16], tb[:, 0:16], u[:, 16:32], ALU.subtract, [rt, rbuf[gi]], [R_d[gi]])
            else:
                pb, rpb = gP()
                mm(pb[:, 0:n], poolw[l][:, gi, :], dT[:, gi, c0_out:c0_out + n], True, True, [R_setup, R_d[gi]], [rpb])
                act(ybT[:, gi, c0_out:c0_out + n], pb[:, 0:n], AF.Copy, [rpb, R_const], [R_yb[gi]],
                    scale=sm[l][:, S_PSC + gi:S_PSC + gi + 1])

    def layer(t, l, Tt, segs, skip_norm=False, after_q=None, before_down=None):
        last = (t == NTILES - 1)
        pseg = segs[0]
        sseg = segs[1] if len(segs) > 1 else None
        nP = pseg.n
        nch = nP // 64
        if not skip_norm:
            norm_phase(Tt, segs, s1[l], 0, l)
        for gq in range(2):
            wb, rw = w_next()
            wv = wb[:].rearrange("p (k c) -> p k c", k=8)
            for f4 in range(4):
                f = gq * 4 + f4
                pb, rpb = proj_fm(wv, rw, f4 * 128, Tt)
                act(qT[:, f, 0:Tt], pb[:, 0:Tt], AF.Copy, [rpb], [R_q[f]], scale=0.125)
            w_done()
            if gq == 1 and after_q is not None:
                after_q()
        cpy("act", KKb[:, :, 0:128], KKst[l][:], [R_KKst[l]], [R_KKb])
        cpy("act", V64b[:, 0:2, :], V64st[l][:], [R_Vst[l]], [R_Vb])
        cpy("act", puTb[:, :, 0:16], pust[l][:], [R_pust[l]], R_pub)
        wb, rw = w_next()
        wv = wb[:].rearrange("p (k c) -> p k c", k=8)
        for ge in range(4):
            pb, rpb = proj_fm(wv, rw, ge * 128, Tt)
            cpy("dve", KKb[:, ge, 128:128 + nP], pb[:, 0:nP], [rpb], [R_KKb])
            if sseg is not None:
                cpy("dve", KKs[:, ge, :], pb[:, sseg.c0:sseg.c0 + 32], [rpb], [R_KKs])
        if last or sseg is not None:
            def krhs(k, wv=wv):
                return wv[:, k, :].rearrange("p (a c) -> p a c", c=256)[:, :, 0:64]
            if last:
                pb, rpb = tok_rows(nP - 128, 128, krhs, 128, rw)
                tb, rt = gF()
                cpy("dve", tb[:, 0:128], pb[:, 0:128], [rpb], [rt])
                dma("sp", so, okp_d[l], tb[:, 0:128], [rt], (), final=True)
            if sseg is not None:
                pb, rpb = tok_rows(sseg.c0, 32, krhs, 128, rw)
                tb, rt = gF()
                cpy("dve", tb[0:32, 0:128], pb[0:32, 0:128], [rpb], [rt])
                dma("sp", so, oks_d[l, 96:128, :], tb[0:32, 0:128], [rt], (), final=True)
        w_done()
        wb, rw = w_next()
        wv = wb[:].rearrange("p (k c) -> p k c", k=8)
        for c2 in range(0, nch, 2):
            pb, rpb = gP()
            for cc in range(2):
                c = c2 + cc
                for k in range(KT):
                    mm(pb[0:64, cc * 256:(cc + 1) * 256], hT[:, k, c * 64:(c + 1) * 64], wv[:, k, 0:256],
                       k == 0, k == KT - 1, [rw, R_h[k]], [rpb])
            cpy("dve", V64b[0:64, 2 + c2:4 + c2, :], pb[0:64, :].rearrange("p (a c) -> p a c", a=2), [rpb], [R_Vb])
        if sseg is not None:
            pb, rpb = gP()
            for k in range(KT):
                mm(pb[0:32, 0:256], hT[:, k, sseg.c0:sseg.c0 + 32], wv[:, k, 0:256], k == 0, k == KT - 1,
                   [rw, R_h[k]], [rpb])
            cpy("dve", V64s[0:32, :], pb[0:32, 0:256], [rpb], [R_Vs])
        if last or sseg is not None:
            def vrhs(k, wv=wv):
                return wv[:, k, 0:256].rearrange("p (a c) -> p a c", c=128)[:, :, 0:64]
            if last:
                pb, rpb = tok_rows(nP - 128, 128, vrhs, 128, rw)
                tb, rt = gF()
                cpy("dve", tb[:, 0:128], pb[:, 0:128], [rpb], [rt])
                dma("sp", so, ovp_d[l], tb[:, 0:128], [rt], (), final=True)
            if sseg is not None:
                pb, rpb = tok_rows(sseg.c0, 32, vrhs, 128, rw)
                tb, rt = gF()
                cpy("dve", tb[0:32, 0:128], pb[0:32, 0:128], [rpb], [rt])
                dma("sp", so, ovs_d[l, 96:128, :], tb[0:32, 0:128], [rt], (), final=True)
        w_done()
        wb, rw = w_next()
        wv = wb[:].rearrange("p (k c) -> p k c", k=8)
        for g4 in range(4):
            pb, rpb = proj_fm(wv, rw, g4 * 128, Tt)
            cpy("act", puTb[:, g4, 16:16 + nP], pb[:, 0:nP], [rpb], [R_pub[g4]])
            if sseg is not None:
                cpy("act", puTs[l][:, g4, 16:48], pb[:, sseg.c0:sseg.c0 + 32], [rpb], [R_pus[l][g4]])
        if last or sseg is not None:
            def prhs(k, wv=wv):
                return wv[:, k, :]
            for (cond, c0r, dst) in ((last, nP - 32, opoolp_d), (sseg is not None, HALO, opools_d)):
                if not cond:
                    continue
                pb, rpb = tok_rows(c0r, 32, prhs, 512, rw)
                tb, rt = gF()
                cpy("dve", tb[0:32, 0:512], pb[0:32, 0:512], [rpb], [rt])
                dma("sp", so, dst[l], tb[17:32, 0:512], [rt], (), final=True)
        w_done()
        wb, rw = w_next()
        wv = wb[:].rearrange("p (k c) -> p k c", k=8)
        for g4 in range(4):
            pb, rpb = proj_fm(wv, rw, g4 * 128, Tt)
            act(uuT[:, g4, 0:Tt], pb[:, 0:Tt], AF.Gelu_apprx_tanh, [rpb], [R_uu[g4]])
        w_done()
        wb, rw = w_next()
        wv = wb[:].rearrange("p (k c) -> p k c", k=8)
        blocks = [(b * 128, 128) for b in range(nP // 128)]
        if sseg is not None:
            blocks.append((sseg.c0, 32))
        gvs = []
        memset("dve", ss4[:], 0.0, [R_ss4])
        for bi, (bc0, bn) in enumerate(blocks):
            pb, rpb = tok_rows(bc0, bn, lambda k, wv=wv: wv[:, k, :], 512, rw)
            gv, rgv = gF()
            act(gv[0:bn, 0:512], pb[0:bn, 0:512], AF.Gelu_apprx_tanh, [rpb], [rgv])
            jb, rj = denb[1], R_den[1]
            act(jb[0:bn, 0:512], gv[0:bn, 0:512], AF.Square, [rgv, R_ss4], [rj, R_ss4], accum_out=ss4[0:bn, bi:bi + 1])
            gvs.append((gv, rgv, bn))
        act(rs4[:, 0:4], ss4[:, 0:4], AF.Ln, [R_ss4, R_const], [R_rs4], bias=epsb[:, 0:1], scale=1.0 / 512)
        act(rs4[:, 0:4], rs4[:, 0:4], AF.Exp, [R_rs4], [R_rs4], scale=-0.5)
        sng = sm[l][:, S_SNG:S_SNG + 512]
        for bi, (gv, rgv, bn) in enumerate(gvs):
            stt(vn[0:bn, bi, :], gv[0:bn, 0:512], rs4[0:bn, bi:bi + 1], sng[0:bn, :], ALU.mult, ALU.mult,
                [rgv, R_rs4, R_const], [R_vn[bi]])
            if sseg is not None and bi == len(gvs) - 1:
                tb, rt = gF()
                stt(tb[0:32, 0:512], gv[0:32, 0:512], rs4[0:32, bi:bi + 1], sng[0:32, :], ALU.mult, ALU.mult,
                    [rgv, R_rs4, R_const], [rt])
                dma("sp", so, osgu_d[l], tb[0:32, 0:512], [rt], (), final=True)
        w_done()
        sgb = sm[l][:, S_SGB:S_SGB + 512].rearrange("p (g i) -> p g i", g=4)
        nblk = nP // 128
        sgu_banks = {}

        def mixers_1():
            pool_mixer(l, puTb, R_pub, nP, 0, t == 1, "A")
            if sseg is not None:
                pool_mixer(l, puTs[l], R_pus[l], 32, sseg.c0, False, "A")

        def mixers_2():
            pool_mixer(l, puTb, R_pub, nP, 0, t == 1, "B")
            if sseg is not None:
                pool_mixer(l, puTs[l], R_pus[l], 32, sseg.c0, False, "B")
            for g4 in range(4):
                if t == 0:
                    tsc(pust[l][:, g4, :], puTb[:, g4, nP:nP + 16], coremask[:, 0:1], None, ALU.mult, None,
                        [R_pub[g4], R_const], [R_pust[l]])
                else:
                    cpy("act", pust[l][:, g4, :], puTb[:, g4, nP:nP + 16], [R_pub[g4]], [R_pust[l]])
            for g4 in range(4):
                pb, rpb = gP()
                for b in range(nblk):
                    mm(pb[:, b * 128:(b + 1) * 128], vn[:, b, g4 * 128:(g4 + 1) * 128], sguw[l][:, g4, :], True, True,
                       [R_vn[b], R_setup], [rpb])
                if sseg is not None:
                    mm(pb[:, nP:nP + 32], vn[0:32, nblk, g4 * 128:(g4 + 1) * 128], sguw[l][0:32, g4, 0:32], True, True,
                       [R_vn[nblk], R_setup], [rpb])
                sgu_banks[g4] = (pb, rpb)

        def mixers_3():
            for g4 in range(4):
                pb, rpb = sgu_banks[g4]
                tb, rt = gF()
                tt(tb[:, 0:nP].rearrange("p (b i) -> p b i", i=128), pb[:, 0:nP].rearrange("p (b i) -> p b i", i=128),
                   sgb[:, g4, :].unsqueeze(1).to_broadcast([128, nblk, 128]), ALU.add, [rpb, R_const], [rt])
                if sseg is not None:
                    tt(tb[:, nP:nP + 32], pb[:, nP:nP + 32], sgb[:, g4, 0:32], ALU.add, [rpb, R_const], [rt])
                tt(uuT[:, g4, 0:Tt], tb[:, 0:Tt], uuT[:, g4, 0:Tt], ALU.mult, [rt, R_uu[g4]], [R_uu[g4]])

        for gg in range(6):
            wb, rw = w_next()
            wv = wb[:].rearrange("p (k c) -> p k c", k=8)
            for f4 in range(4):
                f = gg * 4 + f4
                pb, rpb = proj_fm(wv, rw, f4 * 128, Tt)
                act(G[:, f, 0:Tt], pb[:, 0:Tt], AF.Tanh, [rpb], [R_G[f]], scale=0.5)
            w_done()
            if gg == 0:
                mixers_1()
            elif gg == 1:
                mixers_2()
            elif gg == 2:
                mixers_3()
        its = []
        for c in range(nch):
            for g in range(2):
                pieces = []
                for pi in range(3):
                    kc0 = 64 * (c + pi)
                    masked = (t == 1 and (c + pi) < 2)
                    pieces.append((KKb[:, 2 * g, kc0:kc0 + 64], KKb[:, 2 * g + 1, kc0:kc0 + 64],
                                   V64b[:, c + pi, g * 128:(g + 1) * 128], 64, pi, [R_KKb, R_Vb], masked))
                its.append((g, c * 64, 64, pieces))
        if sseg is not None:
            for g in range(2):
                pieces = []
                for pi in range(2):
                    pieces.append((KKc[l][:, 2 * g, pi * 64:(pi + 1) * 64], KKc[l][:, 2 * g + 1, pi * 64:(pi + 1) * 64],
                                   Vc[l][:, pi, g * 128:(g + 1) * 128], 64, pi, [R_setup], False))
                pieces.append((KKs[:, 2 * g, :], KKs[:, 2 * g + 1, :], V64s[0:32, g * 128:(g + 1) * 128], 32, 2,
                               [R_KKs, R_Vs], False))
                its.append((g, sseg.c0, 32, pieces))
        st1 = {}
        st2 = {}
        nit = len(its)
        for r in range(-1, nit):
            if r + 1 < nit:
                g, qc0, nq, pieces = its[r + 1]
                st1[r + 1] = attn_S(l, g, qc0, nq, pieces)
            if 0 <= r < nit:
                st2[r] = attn_PV(st1.pop(r))
            if r - 1 >= 0:
                attn_norm(st2.pop(r - 1))
        attn_norm(st2.pop(nit - 1))
        assert not st1 and not st2
        cpy("act", KKst[l][:], KKb[:, :, nP:nP + 128], [R_KKb], [R_KKst[l]])
        cpy("act", V64st[l][:], V64b[:, nch:nch + 2, :], [R_Vb], [R_Vst[l]])
        for m in range(4):
            wb, rw = w_next()
            wa = wb[:, 0:2048].rearrange("p (k c) -> p k c", k=8)
            wp = wb[:, 2048:3072].rearrange("p (k c) -> p k c", k=4)
            wsg = wb[:, 3072:4096].rearrange("p (k c) -> p k c", k=4)
            for d2 in range(2):
                dt_ = 2 * m + d2
                cs = slice(d2 * 128, (d2 + 1) * 128)
                pa, rpa = gP()
                for k in range(KT):
                    mm(pa[:, 0:Tt], wa[:, k, cs], yaT[:, k, 0:Tt], k == 0, k == KT - 1, [rw, R_ya[k]], [rpa])
                pp, rpp = gP()
                for k in range(4):
                    mm(pp[:, 0:Tt], wp[:, k, cs], ybT[:, k, 0:Tt], k == 0, k == 3, [rw, R_yb[k]], [rpp])
                pg, rpg = gP()
                for k in range(4):
                    mm(pg[:, 0:Tt], wsg[:, k, cs], uuT[:, k, 0:Tt], k == 0, k == 3, [rw, R_uu[k]], [rpg])
                t1, r1 = gF()
                stt(t1[:, 0:Tt], G[:, dt_, 0:Tt], 1.0, pa[:, 0:Tt], ALU.add, ALU.mult, [R_G[dt_], rpa], [r1])
                t2, r2 = gF()
                stt(t2[:, 0:Tt], G[:, 8 + dt_, 0:Tt], 1.0, pp[:, 0:Tt], ALU.add, ALU.mult, [R_G[8 + dt_], rpp], [r2])
                tt(t1[:, 0:Tt], t1[:, 0:Tt], t2[:, 0:Tt], ALU.add, [r1, r2], [r1])
                t3, r3 = gF()
                stt(t3[:, 0:Tt], G[:, 16 + dt_, 0:Tt], 1.0, pg[:, 0:Tt], ALU.add, ALU.mult, [R_G[16 + dt_], rpg], [r3])
                tt(qT[:, dt_, 0:Tt], t1[:, 0:Tt], t3[:, 0:Tt], ALU.add, [r1, r3], [R_q[dt_]])
            w_done()
        for gq in range(2):
            wb, rw = w_next()
            wv = wb[:].rearrange("p (k c) -> p k c", k=8)
            for f4 in range(4):
                dt_ = gq * 4 + f4
                pb, rpb = gP()
                for k in range(KT):
                    mm(pb[:, 0:Tt], wv[:, k, f4 * 128:(f4 + 1) * 128], qT[:, k, 0:Tt], k == 0, k == KT - 1,
                       [rw, R_q[k]], [rpb])
                for sg in segs:
                    cs = slice(sg.c0, sg.c0 + sg.n)
                    stt(xT[:, dt_, cs], pb[:, cs], g1h[l][:, dt_, sg.r:sg.r + 1], xT[:, dt_, cs], ALU.mult, ALU.add,
                        [rpb, R_mod[l], R_x[dt_]], [R_x[dt_]])
            w_done()
        norm_phase(Tt, segs, s2[l], 24, l)
        cw = sm[l][:, S_CW:S_CW + 66].rearrange("p (f j) -> p f j", j=3)
        cb = sm[l][:, S_CB:S_CB + 22]
        pipe = (len(segs) == 1)
        stY = {}
        stZ = {}

        def ffn_X(f, pa, rpa, pbk, rpbk):
            lst = []
            for sg in segs:
                ab, rab = gF()
                n = sg.n
                if sg.kind == "P":
                    cpy("act", ab[:, 0:2], aprev[l][:, f, :], [R_ap[l]], [rab])
                else:
                    cpy("act", ab[:, 0:2], aprev_s[l][:, f, :], [R_aps[l]], [rab])
                cpy("act", ab[:, 2:2 + n], pa[:, sg.c0:sg.c0 + n], [rpa], [rab])
                if sg.kind == "P":
                    if t == 0:
                        tsc(aprev[l][:, f, :], ab[:, n:n + 2], coremask[:, 0:1], None, ALU.mult, None,
                            [rab, R_const], [R_ap[l]])
                    else:
                        cpy("dve", aprev[l][:, f, :], ab[:, n:n + 2], [rab], [R_ap[l]])
                acc, racc = gF()
                act(acc[:, 0:n], ab[:, 0:n], AF.Copy, [rab, R_const], [racc], scale=cw[:, f, 0:1])
                stt(acc[:, 0:n], ab[:, 1:1 + n], cw[:, f, 1:2], acc[:, 0:n], ALU.mult, ALU.add, [rab, racc, R_const], [racc])
                stt(acc[:, 0:n], ab[:, 2:2 + n], cw[:, f, 2:3], acc[:, 0:n], ALU.mult, ALU.add, [rab, racc, R_const], [racc])
                lst.append((sg, acc, racc))
            stY[f] = (lst, pbk, rpbk)

        def ffn_Y(f):
            lst, pbk, rpbk = stY.pop(f)
            for (sg, acc, racc) in lst:
                act(acc[:, 0:sg.n], acc[:, 0:sg.n], AF.Gelu_apprx_tanh, [racc, R_const], [racc], bias=cb[:, f:f + 1])
            stZ[f] = (lst, pbk, rpbk)

        def ffn_Z(f):
            lst, pbk, rpbk = stZ.pop(f)
            for (sg, acc, racc) in lst:
                tt(G[:, f, sg.c0:sg.c0 + sg.n], acc[:, 0:sg.n], pbk[:, sg.c0:sg.c0 + sg.n], ALU.mult, [racc, rpbk], [R_G[f]])

        dcnt = [0]
        for j in range(11):
            wb, rw = w_next()
            wv = wb[:].rearrange("p (k c) -> p k c", k=8)
            for fi in range(2):
                f = 2 * j + fi
                pa, rpa = proj_fm(wv, rw, fi * 128, Tt)
                pbk, rpbk = proj_fm(wv, rw, 256 + fi * 128, Tt)
                ffn_X(f, pa, rpa, pbk, rpbk)
                if pipe:
                    if f - 1 >= 0:
                        ffn_Y(f - 1)
                    if f - 2 >= 0:
                        ffn_Z(f - 2)
                else:
                    ffn_Y(f)
                    ffn_Z(f)
            if last or sseg is not None:
                for (cond, c0r, dst) in ((last, nP - 32, oconvp_d), (sseg is not None, HALO, oconvs_d)):
                    if not cond:
                        continue
                    pb, rpb = tok_rows(c0r, 32, lambda k, wv=wv: wv[:, k, 0:256], 256, rw)
                    di = dcnt[0] % 3
                    dcnt[0] += 1
                    tb, rt = denb[di], R_den[di]
                    cpy("dve", tb[0:32, 0:256], pb[0:32, 0:256], [rpb], [rt])
                    dma("sp", so, dst[l, :, 256 * j:256 * (j + 1)], tb[30:32, 0:256], [rt], (), final=True)
            w_done()
        if pipe:
            ffn_Y(FT - 1)
            ffn_Z(FT - 2)
            ffn_Z(FT - 1)
        assert not stY and not stZ
        if before_down is not None:
            before_down()
        for m in range(6):
            wb, rw = w_next()
            wv = wb[:].rearrange("p (k c) -> p k c", k=4)
            nf = 4 if m < 5 else 2
            for dt_ in range(KT):
                for fi in range(nf):
                    f = 4 * m + fi
                    mm(psum[dt_][:, 0:Tt], wv[:, fi, dt_ * 128:(dt_ + 1) * 128], G[:, f, 0:Tt], f == 0, f == FT - 1,
                       [rw, R_G[f]], [R_ps[dt_]])
            w_done()
        for dt_ in range(KT):
            for sg in segs:
                cs = slice(sg.c0, sg.c0 + sg.n)
                stt(xT[:, dt_, cs], psum[dt_][:, cs], modT[l][:, 40 + dt_, sg.r:sg.r + 1], xT[:, dt_, cs], ALU.mult, ALU.add,
                    [R_ps[dt_], R_mod[l], R_x[dt_]], [R_x[dt_]])

    def final_phase(t, Tt, segs):
        for k in range(KT):
            act(x2[:, k, 0:Tt], xT[:, k, 0:Tt], AF.Square, [R_x[k]], [R_x2[k]])
        pb, rpb = gP()
        for k in range(KT):
            mm(pb[:, 0:Tt], ones_bf[:], x2[:, k, 0:Tt], k == 0, k == KT - 1, [R_x2[k], R_const], [rpb])
        rb, rr = gF()
        act(rb[:, 0:Tt], pb[:, 0:Tt], AF.Ln, [rpb, R_const], [rr], bias=epsb[:, 0:1], scale=1.0 / D)
        act(rb[:, 0:Tt], rb[:, 0:Tt], AF.Exp, [rr], [rr], scale=-0.5)
        for k in range(KT):
            stt(xT[:, k, 0:Tt], xT[:, k, 0:Tt], fng[:, k:k + 1], rb[:, 0:Tt], ALU.mult, ALU.mult,
                [R_x[k], rr, R_const], [R_x[k]])
        if t == 0:
            dma("sp", so, ysT_d.rearrange("(k p) t -> p k t", p=128), xT[:, :, HALO:HALO + 32], R_x, (), final=True)
        else:
            c0 = (t - 1) * 512
            dma("sp", so, yT_d.rearrange("(k p) t -> p k t", p=128)[:, :, c0:c0 + 512], xT[:, :, 0:512], R_x, (), final=True)

    XT = [xTa, xTb]
    RX = [R_xa, R_xb]
    xview = xTp.rearrange("(k p) t -> p k t", p=128)

    def load_x(t):
        buf, rbuf = XT[t % 2], RX[t % 2]
        if t == 0:
            dma("sp", xslot, buf[:, :, 0:HALO], xview[:, :, 0:HALO], (), rbuf)
            dma("sp", xslot, buf[:, :, HALO:HALO + 32], xTs.rearrange("(k p) t -> p k t", p=128), (), rbuf)
        else:
            c0 = HALO + (t - 1) * 512
            dma("sp", xslot, buf[:, :, 0:512], xview[:, :, c0:c0 + 512], (), rbuf)

    def with_x(t, fn):
        nonlocal xT, R_x
        sx, sr = xT, R_x
        xT, R_x = XT[t % 2], RX[t % 2]
        try:
            fn()
        finally:
            xT, R_x = sx, sr

    def tsz(t):
        sg_ = tile_segs(t)
        return sum(s_.n for s_ in sg_), sg_

    load_x(0)
    Tt0, segs0 = tsz(0)
    with_x(0, lambda: norm_phase(Tt0, segs0, s1[0], 0, 0))
    for t in range(NTILES):
        Tt, segs = tsz(t)
        aq = None
        if t >= 1:
            Tp, sp_ = tsz(t - 1)
            aq = (lambda tp=t - 1, Tp=Tp, sp_=sp_: with_x(tp, lambda: final_phase(tp, Tp, sp_)))
        ilv["active"] = (t == 0)
        with_x(t, lambda: layer(t, 0, Tt, segs, skip_norm=True, after_q=aq))
        ilv["active"] = False
        bd = None
        if t + 1 < NTILES:
            load_x(t + 1)
            Tn, sn_ = tsz(t + 1)
            bd = (lambda tn=t + 1, Tn=Tn, sn_=sn_: with_x(tn, lambda: norm_phase(Tn, sn_, s1[0], 0, 0)))
        with_x(t, lambda: layer(t, 1, Tt, segs, before_down=bd))
    TL, sL = tsz(NTILES - 1)
    with_x(NTILES - 1, lambda: final_phase(NTILES - 1, TL, sL))

    assert wstate["used"] == len(sched), (wstate, len(sched))
    P.emit(nc, st)
    st.close()
    return nc


def _km(W, k, ncols):
    c = W.shape[1]
    out = np.zeros((128, k, ncols), np.float32)
    out[:, :, :c] = W.reshape(k, 128, c).transpose(1, 0, 2)
    return out.reshape(128, k * ncols)


def _weights(inp):
    wl = np.zeros((2, NG_L, 128, 4096), np.float32)
    wada = np.zeros((2, 12, 128, 4096), np.float32)
    for l in range(2):
        w_in = np.asarray(inp["w_in"][l], np.float32)
        for g in range(12):
            wada[l, g] = _km(np.asarray(inp["w_ada"][l][:, g * 512:(g + 1) * 512], np.float32), 8, 512)
        gi = 0
        for gq in range(2):
            wl[l, gi] = _km(w_in[:, gq * 512:(gq + 1) * 512], 8, 512)
            gi += 1
        wk = w_in[:, 1024:1152]
        kz = np.zeros((1024, 512), np.float32)
        for g in range(2):
            kz[:, (2 * g) * 128:(2 * g) * 128 + 64] = wk[:, g * 64:(g + 1) * 64]
            kz[:, (2 * g + 1) * 128 + 64:(2 * g + 1) * 128 + 128] = wk[:, g * 64:(g + 1) * 64]
        wl[l, gi] = _km(kz, 8, 512)
        gi += 1
        wv = w_in[:, 1152:1280]
        vd = np.zeros((1024, 256), np.float32)
        for g in range(2):
            vd[:, g * 128:g * 128 + 64] = wv[:, g * 64:(g + 1) * 64]
            vd[:, g * 128 + 64:g * 128 + 128] = wv[:, g * 64:(g + 1) * 64]
        wl[l, gi] = _km(vd, 8, 512)
        gi += 1
        for c0 in (1280, 1792, 2304):
            wl[l, gi] = _km(w_in[:, c0:c0 + 512], 8, 512)
            gi += 1
        for gg in range(6):
            c0 = 2816 + gg * 512
            wl[l, gi] = _km(w_in[:, c0:c0 + 512], 8, 512)
            gi += 1
        woa = np.asarray(inp["w_o_attn"][l], np.float32)
        wop = np.asarray(inp["w_o_pool"][l], np.float32)
        wos = np.asarray(inp["w_o_sgu"][l], np.float32)
        for m in range(4):
            cs = slice(m * 256, (m + 1) * 256)
            wl[l, gi, :, 0:2048] = _km(woa[:, cs], 8, 256)
            wl[l, gi, :, 2048:3072] = _km(wop[:, cs], 4, 256)
            wl[l, gi, :, 3072:4096] = _km(wos[:, cs], 4, 256)
            gi += 1
        wout = np.asarray(inp["w_out"][l], np.float32)
        for gq in range(2):
            wl[l, gi] = _km(wout[:, gq * 512:(gq + 1) * 512], 8, 512)
            gi += 1
        wup = np.asarray(inp["ffn_w_up"][l], np.float32)
        for j in range(11):
            blk = np.concatenate([wup[:, 256 * j:256 * (j + 1)], wup[:, DFF + 256 * j:DFF + 256 * (j + 1)]], axis=1)
            wl[l, gi] = _km(blk, 8, 512)
            gi += 1
        wdn = np.asarray(inp["ffn_w_down"][l], np.float32)
        for m in range(6):
            nf = 4 if m < 5 else 2
            wl[l, gi, :, 0:nf * 1024] = _km(wdn[m * 512:m * 512 + nf * 128, :], nf, 1024)
            gi += 1
        assert gi == NG_L
    return wl, wada


def _pk(v, k):
    return np.asarray(v, np.float32).reshape(k, 128).T


def _smalls(inp):
    sm = np.zeros((2, 128, NS), np.float32)
    for l in range(2):
        sm[l, :, S_N1G:S_N1G + 8] = _pk(inp["norm1_g"][l], 8)
        sm[l, :, S_N2G:S_N2G + 8] = _pk(inp["norm2_g"][l], 8)
        sm[l, :, S_PSC:S_PSC + 4] = _pk(inp["pool_scale"][l], 4)
        cw = np.asarray(inp["ffn_conv_w"][l], np.float32)
        sm[l, :, S_CW:S_CW + 66] = cw.reshape(3, FT, 128).transpose(2, 1, 0).reshape(128, 66)
        sm[l, :, S_CB:S_CB + 22] = _pk(inp["ffn_conv_b"][l], FT)
        sm[l, :, S_BADA:S_BADA + 48] = _pk(inp["b_ada"][l], 48)
        sink = np.asarray(inp["attn_sink"][l], np.float32)
        order = [8 * g + 2 * i + e for g in range(2) for e in range(2) for i in range(4)]
        sm[l, :, S_SINK:S_SINK + 16] = sink[order][None, :]
        sm[l, :, S_SNG:S_SNG + 512] = np.asarray(inp["sgu_norm_g"][l], np.float32)[None, :]
        sm[l, :, S_SGB:S_SGB + 512] = np.asarray(inp["sgu_b"][l], np.float32).reshape(1, 512)
    return sm


def _consts():
    h = np.arange(1, 17, dtype=np.float64)
    slopes = np.exp2(-8.0 * h / 16)
    i = np.arange(64)[:, None]
    j = np.arange(64)[None, :]
    eb = np.zeros((64, 6, 2, 4, 64), np.float64)
    for kind, dist in enumerate((j + 128 - i, j + 64 - i, np.abs(j - i))):
        for g in range(2):
            for e in range(2):
                for ii in range(4):
                    hh = 8 * g + 2 * ii + e
                    eb[:, kind * 2 + g, e, ii, :] = np.exp(-slopes[hh] * dist)
    eb = eb.reshape(64, 6, 512).astype(np.float32)
    a = np.arange(128)
    sgumask = ((a[:, None] // 64) <= (a[None, :] // 64)).astype(np.float32)
    return eb, sgumask


_CACHE = {}


def kernel(**inp):
    if "nc" not in _CACHE:
        _CACHE["nc"] = build_program()
    nc = _CACHE["nc"]
    wl, wada = _weights(inp)
    smalls = _smalls(inp)
    eb, sgumask = _consts()
    fng = _pk(inp["final_norm_g"], 8)
    poolw = np.stack([np.asarray(inp["pool_w"][l], np.float32).transpose(1, 0, 2) for l in range(2)])
    sguwT = np.stack([np.asarray(inp["sgu_w"][l], np.float32).transpose(2, 0, 1) for l in range(2)])
    xp = np.asarray(inp["x_prompt"], np.float32)
    xs = np.asarray(inp["x_sample"], np.float32)
    cp = np.asarray(inp["c_prompt"], np.float32)
    cs_ = np.asarray(inp["c_sample"], np.float32)
    ck = np.asarray(inp["cache_k_win"], np.float32)
    cv = np.asarray(inp["cache_v_win"], np.float32)
    spool = np.asarray(inp["state_pool"], np.float32)
    sconv = np.asarray(inp["state_ffn_conv"], np.float32)
    in_maps = []
    for c in range(8):
        b, half = c // 2, c % 2
        S = half * HALF
        xT = np.zeros((D, NTOK), np.float32)
        lo = S - HALO
        if lo < 0:
            xT[:, HALO:] = xp[b, 0:HALF].T
        else:
            xT[:, :] = xp[b, lo:S + HALF].T
        cvec = np.stack([cp[b], cs_[c]], axis=1)
        cT = cvec.reshape(8, 128, 2).transpose(1, 0, 2)
        coremask = np.full((128, 1), float(half), np.float32)
        poolinv = np.zeros((128, 4, 16), np.float32)
        for gi, w in enumerate((2, 4, 8, 16)):
            pos = S + np.arange(16)
            poolinv[:, gi, :] = (1.0 / np.minimum(pos + 1, w))[None, :]
        kcz = np.zeros((2, 128, 4, 128), np.float32)
        vcd = np.zeros((2, 64, 2, 256), np.float32)
        for l in range(2):
            for g in range(2):
                kt_ = ck[l, c, :, g, :].T
                kcz[l, 0:64, 2 * g, :] = kt_
                kcz[l, 64:128, 2 * g + 1, :] = kt_
                for pi in range(2):
                    vv = cv[l, c, pi * 64:(pi + 1) * 64, g, :]
                    vcd[l, :, pi, g * 128:g * 128 + 64] = vv
                    vcd[l, :, pi, g * 128 + 64:g * 128 + 128] = vv
        spoolT = spool[:, c].reshape(2, 15, 4, 128).transpose(0, 3, 2, 1)
        sconvT = sconv[:, c].reshape(2, 2, FT, 128).transpose(0, 3, 2, 1)
        in_maps.append({
            "xTp": xT, "xTs": np.ascontiguousarray(xs[c].T), "cT": np.ascontiguousarray(cT),
            "wl": wl, "wada": wada, "smalls": smalls, "fng": np.ascontiguousarray(fng),
            "poolw": np.ascontiguousarray(poolw), "sguwT": np.ascontiguousarray(sguwT), "sgumask": sgumask,
            "eb": eb, "coremask": coremask, "poolinv": poolinv, "kcz": kcz, "vcd": vcd,
            "cktok": np.ascontiguousarray(ck[:, c].reshape(2, 128, 128)),
            "cvtok": np.ascontiguousarray(cv[:, c].reshape(2, 128, 128)),
            "spoolT": np.ascontiguousarray(spoolT), "sconvT": np.ascontiguousarray(sconvT),
        })
    res = run_bass_kernel_spmd(nc, in_maps, core_ids=list(range(8))).results
    y_prompt = np.zeros((4, 8192, D), np.float32)
    y_sample = np.zeros((8, 32, D), np.float32)
    kp = np.zeros((2, 4, 128, 2, 64), np.float32)
    vp = np.zeros_like(kp)
    pp = np.zeros((2, 4, 15, 512), np.float32)
    cpv = np.zeros((2, 4, 2, DFF), np.float32)
    ks = np.zeros((2, 8, 128, 2, 64), np.float32)
    vs = np.zeros_like(ks)
    ps_ = np.zeros((2, 8, 15, 512), np.float32)
    cs2 = np.zeros((2, 8, 2, DFF), np.float32)
    sg = np.zeros((2, 8, 32, 512), np.float32)
    for c in range(8):
        b, half = c // 2, c % 2
        r = res[c]
        y_prompt[b, half * HALF:(half + 1) * HALF] = r["yT"].T
        y_sample[c] = r["ysT"].T
        if half == 1:
            kp[:, b] = r["okp"].reshape(2, 128, 2, 64)
            vp[:, b] = r["ovp"].reshape(2, 128, 2, 64)
            pp[:, b] = r["opoolp"]
            cpv[:, b] = r["oconvp"]
        ks[:, c] = r["oks"].reshape(2, 128, 2, 64)
        vs[:, c] = r["ovs"].reshape(2, 128, 2, 64)
        ps_[:, c] = r["opools"]
        cs2[:, c] = r["oconvs"]
        sg[:, c] = r["osgu"]
    return (y_prompt, y_sample, kp, vp, pp, cpv, ks, vs, ps_, cs2, sg)
```
